# Optimizing a Trainium2 kernel written in Bass

```python
import jax, jax.numpy as jnp
from jax import lax
import numpy as np

D_MODEL = 2048
BATCH = 1
SEQ = 16384
DEPTH = 2
DEC_BATCH = 32
DEC_SEQ = 32
PAST_LEN = 4096

CHUNK = 64
GROUP_W = D_MODEL // 16
POOL_WINDOWS = (2, 4, 8, 16)
POOL_GROUPS = len(POOL_WINDOWS)
POOL_W = POOL_GROUPS * GROUP_W
POOL_HIST = max(POOL_WINDOWS) - 1
SGU_HEADS = 6
SGU_W = SGU_HEADS * GROUP_W
SGU_CHUNK = 128
CONV_GROUPS = 6
CONV_W = CONV_GROUPS * GROUP_W
CONV_K = 31
CONV_HIST = CONV_K - 1
MIX_W = POOL_W + SGU_W + CONV_W
IN_W = POOL_W + 2 * SGU_W + 2 * CONV_W
D_FF = ((8 * D_MODEL + 3 * 256 - 1) // (3 * 256)) * 256
RMS_EPS = 1e-6
LN_EPS = 1e-5

kernel_name = "hybrid_pool_sgu_conformer_stream_step"


def rms_norm(x, g):
    xf = x.astype(jnp.float32)
    y = xf * lax.rsqrt(jnp.mean(xf * xf, axis=-1, keepdims=True) + RMS_EPS)
    return (y * g.astype(jnp.float32)).astype(x.dtype)


def layer_norm(x, g, b):
    xf = x.astype(jnp.float32)
    mu = jnp.mean(xf, axis=-1, keepdims=True)
    xc = xf - mu
    y = xc * lax.rsqrt(jnp.mean(xc * xc, axis=-1, keepdims=True) + LN_EPS)
    return (y * g.astype(jnp.float32) + b.astype(jnp.float32)).astype(x.dtype)


def pool_mixer(xa, hist, pos0, pool_w, pool_scale):
    B, S, _ = xa.shape
    ext = jnp.concatenate([hist, xa], axis=1)
    cs = jnp.cumsum(ext.astype(jnp.float32), axis=1)
    cs = jnp.pad(cs, ((0, 0), (1, 0), (0, 0)))
    pos = pos0 + jnp.arange(S)
    means = []
    for g, w in enumerate(POOL_WINDOWS):
        sl = slice(g * GROUP_W, (g + 1) * GROUP_W)
        s = cs[:, POOL_HIST + 1:POOL_HIST + 1 + S, sl] - cs[:, POOL_HIST + 1 - w:POOL_HIST + 1 - w + S, sl]
        cnt = jnp.minimum(pos + 1, w).astype(jnp.float32)[None, :, None]
        means.append(s / cnt)
    pooled = jnp.concatenate(means, axis=-1).astype(xa.dtype) - xa
    pg = pooled.reshape(B, S, POOL_GROUPS, GROUP_W)
    out = jnp.einsum('bsgc,gcd->bsgd', pg, pool_w).reshape(B, S, POOL_W)
    return out * pool_scale, ext[:, -POOL_HIST:]


def sgu_mixer(u, v, norm_g, ws, bias):
    B, S, _ = u.shape
    v = rms_norm(v, norm_g)
    pad = (-S) % SGU_CHUNK
    vp = jnp.pad(v, ((0, 0), (0, pad), (0, 0)))
    n = (S + pad) // SGU_CHUNK
    vc = vp.reshape(B, n, SGU_CHUNK, SGU_HEADS, GROUP_W)
    idx = jnp.arange(SGU_CHUNK)
    mask = (idx[None, :] // CHUNK) <= (idx[:, None] // CHUNK)
    wm = jnp.where(mask[None], ws, jnp.zeros_like(ws))
    mixed = jnp.einsum('hij,bnjhc->bnihc', wm, vc) + bias.T[None, None, :, :, None]
    mixed = mixed.reshape(B, n * SGU_CHUNK, SGU_W)[:, :S]
    return u * mixed, v


def conv_mixer(a, gate, hist, conv_w, conv_b, ln_g, ln_b):
    h = a * jax.nn.sigmoid(gate)
    ext = jnp.concatenate([hist, h], axis=1)
    y = lax.conv_general_dilated(ext, conv_w[:, None, :].astype(ext.dtype), window_strides=(1,),
                                 padding='VALID', dimension_numbers=('NWC', 'WIO', 'NWC'),
                                 feature_group_count=CONV_W) + conv_b
    y = jax.nn.silu(layer_norm(y, ln_g, ln_b))
    return y, ext[:, -CONV_HIST:]


def trunk_layer(x, pool_hist, conv_hist, pos0, g_pre_mix, g_post_mix, g_pre_ffn, g_post_ffn,
                w_in, w_out, pool_w, pool_scale, sgu_norm, sgu_ws, sgu_b,
                conv_w, conv_b, conv_ln_g, conv_ln_b, w_gate, w_up, w_down):
    h = rms_norm(x, g_pre_mix)
    z = h @ w_in
    xa, u, v, ca, cg = jnp.split(z, [POOL_W, POOL_W + SGU_W, POOL_W + 2 * SGU_W,
                                     POOL_W + 2 * SGU_W + CONV_W], axis=-1)
    ya, new_pool = pool_mixer(xa, pool_hist, pos0, pool_w, pool_scale)
    yb, v_rows = sgu_mixer(u, v, sgu_norm, sgu_ws, sgu_b)
    yc, new_conv = conv_mixer(ca, cg, conv_hist, conv_w, conv_b, conv_ln_g, conv_ln_b)
    mix = jnp.concatenate([ya, yb, yc], axis=-1) @ w_out
    x = x + rms_norm(mix, g_post_mix)
    f = rms_norm(x, g_pre_ffn)
    f = (jax.nn.silu(f @ w_gate) * (f @ w_up)) @ w_down
    x = x + rms_norm(f, g_post_ffn)
    return x, new_pool, new_conv, v_rows


def setup_inputs(seed: int = 0) -> dict:
    key = jax.random.key(seed)
    ks = jax.random.split(key, 24)
    f32 = jnp.float32
    nrm = lambda k, shape, s: jax.random.normal(k, shape, f32) * s
    gain = lambda k, shape: 1.0 + 0.05 * jax.random.normal(k, shape, f32)
    return {
        "x_prompt": jax.random.normal(ks[0], (BATCH, SEQ, D_MODEL), f32),
        "x_sample": jax.random.normal(ks[1], (DEC_BATCH, DEC_SEQ, D_MODEL), f32),
        "state_pool": jax.random.normal(ks[2], (DEPTH, DEC_BATCH, POOL_HIST, POOL_W), f32),
        "state_conv": jax.random.normal(ks[3], (DEPTH, DEC_BATCH, CONV_HIST, CONV_W), f32),
        "norm_pre_mix": gain(ks[4], (DEPTH, D_MODEL)),
        "norm_post_mix": gain(ks[5], (DEPTH, D_MODEL)),
        "norm_pre_ffn": gain(ks[6], (DEPTH, D_MODEL)),
        "norm_post_ffn": gain(ks[7], (DEPTH, D_MODEL)),
        "w_in": nrm(ks[8], (DEPTH, D_MODEL, IN_W), D_MODEL ** -0.5),
        "w_out": nrm(ks[9], (DEPTH, MIX_W, D_MODEL), MIX_W ** -0.5),
        "pool_w": nrm(ks[10], (DEPTH, POOL_GROUPS, GROUP_W, GROUP_W), GROUP_W ** -0.5),
        "pool_scale": gain(ks[11], (DEPTH, POOL_W)),
        "sgu_norm": gain(ks[12], (DEPTH, SGU_W)),
        "sgu_ws": nrm(ks[13], (DEPTH, SGU_HEADS, SGU_CHUNK, SGU_CHUNK), SGU_CHUNK ** -0.5),
        "sgu_b": 1.0 + nrm(ks[14], (DEPTH, SGU_HEADS, SGU_CHUNK), 0.01),
        "conv_w": nrm(ks[15], (DEPTH, CONV_K, CONV_W), CONV_K ** -0.5),
        "conv_b": nrm(ks[16], (DEPTH, CONV_W), 0.01),
        "conv_ln_g": gain(ks[17], (DEPTH, CONV_W)),
        "conv_ln_b": nrm(ks[18], (DEPTH, CONV_W), 0.01),
        "w_gate": nrm(ks[19], (DEPTH, D_MODEL, D_FF), D_MODEL ** -0.5),
        "w_up": nrm(ks[20], (DEPTH, D_MODEL, D_FF), D_MODEL ** -0.5),
        "w_down": nrm(ks[21], (DEPTH, D_FF, D_MODEL), D_FF ** -0.5),
    }


def reference(x_prompt, x_sample, state_pool, state_conv, norm_pre_mix, norm_post_mix, norm_pre_ffn,
              norm_post_ffn, w_in, w_out, pool_w, pool_scale, sgu_norm, sgu_ws, sgu_b,
              conv_w, conv_b, conv_ln_g, conv_ln_b, w_gate, w_up, w_down):
    bp = x_prompt.shape[0]
    hp, hs = x_prompt, x_sample
    pool_p, conv_p, pool_s, conv_s, vrows_s = [], [], [], [], []
    for l in range(DEPTH):
        params = (norm_pre_mix[l], norm_post_mix[l], norm_pre_ffn[l], norm_post_ffn[l],
                  w_in[l], w_out[l], pool_w[l], pool_scale[l], sgu_norm[l], sgu_ws[l], sgu_b[l],
                  conv_w[l], conv_b[l], conv_ln_g[l], conv_ln_b[l], w_gate[l], w_up[l], w_down[l])
        zp = jnp.zeros((bp, POOL_HIST, POOL_W), hp.dtype)
        zc = jnp.zeros((bp, CONV_HIST, CONV_W), hp.dtype)
        hp, npool, nconv, _ = trunk_layer(hp, zp, zc, 0, *params)
        pool_p.append(npool)
        conv_p.append(nconv)
        hs, npool_s, nconv_s, v_s = trunk_layer(hs, state_pool[l].astype(hs.dtype),
                                                state_conv[l].astype(hs.dtype), PAST_LEN, *params)
        pool_s.append(npool_s)
        conv_s.append(nconv_s)
        vrows_s.append(v_s)
    return (hp, hs, jnp.stack(pool_p), jnp.stack(conv_p), jnp.stack(pool_s), jnp.stack(conv_s), jnp.stack(vrows_s))
```

```python
from contextlib import ExitStack
import os
_DBG = os.environ.get('KDBG', '')
import numpy as np
import concourse.bass as bass
import concourse.mybir as mybir
from concourse.bass_utils import run_bass_kernel_spmd

F32 = mybir.dt.float32
BF16 = mybir.dt.bfloat16
AF = mybir.ActivationFunctionType
ALU = mybir.AluOpType

NCORES = 8
D = 2048
INW = 3584
DFF = 5632
NKC = 16
NFC = 44
DEPTH = 2
NT = 18
HALO_T, LASTP_T, SAMPLE_T = 0, 16, 17
GROUPS = [[0, 1, 2, 3], [4, 5, 6, 7], [8, 9, 10, 11], [12, 13, 14, 15], [16, 17]]
RMS_EPS = 1e-6
LN_EPS = 1e-5
NSLOT = 3
SLOTW = 4096
NDSEM = 40


class Sched:
    def __init__(self, nc, stack):
        self.nc = nc
        self.stack = stack
        self.engs = ["pe", "act", "dve", "pool", "sp"]
        self.ops = {e: [] for e in self.engs}
        self.semc = 0
        self.msem = {}
        for e in ("pe", "act", "dve", "pool"):
            self._new_msem(e)
        self.dsems = [stack.enter_context(nc.semaphore(f"dq{i}")) for i in range(NDSEM)]
        self.dcnt = [0] * NDSEM
        self.drr = 0
        self.wsems = [stack.enter_context(nc.semaphore(f"wq{i}")) for i in range(NSLOT)]
        self.wcnt = [0] * NSLOT
        self.lastw = {}
        self.readers = {}
        self.all_dma = {}

    def _new_msem(self, e):
        s = self.stack.enter_context(self.nc.semaphore(f"m{e}{self.semc}"))
        self.semc += 1
        self.msem[e] = [s, 0]

    def new_phase(self):
        for e in ("pe", "act", "dve"):
            self._new_msem(e)

    @staticmethod
    def _add(d, ev):
        s, v = ev
        if d.get(s, 0) < v:
            d[s] = v

    def op(self, eng, fn, reads=(), writes=(), dma=0, wslot=None, selfsync=False):
        deps = {}
        for k in reads:
            ev = self.lastw.get(k)
            if ev is not None:
                self._add(deps, ev)
        for k in writes:
            ev = self.lastw.get(k)
            if ev is not None:
                self._add(deps, ev)
            for s, v in self.readers.get(k, {}).items():
                self._add(deps, (s, v))
        if dma:
            if wslot is not None:
                sem = self.wsems[wslot]
                self.wcnt[wslot] += 16 * dma
                val = self.wcnt[wslot]
            else:
                i = self.drr
                self.drr = (self.drr + 1) % NDSEM
                sem = self.dsems[i]
                if self.dcnt[i] > 0:
                    self._add(deps, (sem, self.dcnt[i]))
                self.dcnt[i] += 16 * dma
                val = self.dcnt[i]
            ev = (sem, val)
            self.all_dma[sem] = val
        else:
            ms = self.msem[eng]
            own = ms[0]
            if eng == "pe":
                for s in [s for s in deps if s.name.startswith("m" + eng)]:
                    del deps[s]
            ms[1] += 1
            ev = (own, ms[1])
        self.ops[eng].append((deps, fn, ev, bool(dma)))
        for k in writes:
            self.lastw[k] = ev
            self.readers[k] = {}
        for k in reads:
            self._add(self.readers.setdefault(k, {}), ev)
        return ev

    def alias(self, old_keys, new_keys):
        evs = {}
        for k in old_keys:
            ev = self.lastw.get(k)
            if ev is not None:
                self._add(evs, ev)
            for s, v in self.readers.get(k, {}).items():
                self._add(evs, (s, v))
        for k in new_keys:
            r = self.readers.setdefault(k, {})
            for s, v in evs.items():
                self._add(r, (s, v))

    def emit(self, eng_name, eng):
        seen = {}
        for deps, fn, ev, is_dma in self.ops[eng_name]:
            for s, v in deps.items():
                if seen.get(s, 0) < v:
                    eng.wait_ge(s, v)
                    seen[s] = v
            r = fn(eng)
            if is_dma:
                for ins in r:
                    ins.then_inc(ev[0], 16)
            else:
                r.then_inc(ev[0], 1)


class _StopBuild(Exception):
    pass


def build_program(groups=None, stop_after=None):
    groups = GROUPS if groups is None else groups

    def stage(k):
        if stop_after is not None and k > stop_after:
            raise _StopBuild()
    nc = bass.Bass("TRN2", target_bir_lowering=False)

    def din(name, shape):
        return nc.dram_tensor(name, list(shape), F32, kind="ExternalInput").ap()

    def dout(name, shape):
        return nc.dram_tensor(name, list(shape), F32, kind="ExternalOutput").ap()

    xin = din("xin", [NT, 128, D])
    flag_d = din("flag", [128, 1])
    spool_d = din("spool", [DEPTH, 60, 512])
    sconv_d = din("sconv", [DEPTH, 120, 768])
    pmat_d = din("pmat", [128, 20, 128])
    ident_d = din("ident", [128, 128])
    w_in_d = din("w_in", [DEPTH, D, INW])
    w_out_d = din("w_out", [DEPTH, D, D])
    w_gate_d = din("w_gate", [DEPTH, D, DFF])
    w_up_d = din("w_up", [DEPTH, D, DFF])
    w_down_d = din("w_down", [DEPTH, DFF, D])
    g4_d = din("g4", [8, D])
    pool_w_d = din("pool_w", [DEPTH, 4, 128, 128])
    vecs_d = din("vecs", [DEPTH, 4, 768])
    sgu_norm_d = din("sgu_norm", [DEPTH, 768])
    sgu_ws_d = din("sgu_ws", [DEPTH, 6, 128, 128])
    sgu_b_d = din("sgu_b", [DEPTH, 6, 128])
    conv_w_d = din("conv_w", [DEPTH, 31, 768])

    yp_d = dout("yp", [16, 128, D])
    ys_d = dout("ys", [128, D])
    npool_p_d = dout("npool_p", [DEPTH, 128, 512])
    nconv_p_d = dout("nconv_p", [DEPTH, 128, 768])
    npool_s_d = dout("npool_s", [DEPTH, 128, 512])
    nconv_s_d = dout("nconv_s", [DEPTH, 128, 768])
    nv_s_d = dout("nv_s", [DEPTH, 128, 768])

    def sb(name, shape, dt):
        return nc.alloc_sbuf_tensor(name, list(shape), dt)

    xbuf = sb("xbuf", [128, 4, D], F32)
    blkB = sb("blkB", [128, 8192], F32)
    A_WORDS = 14852 + 512
    blkA = sb("blkA", [128, A_WORDS], F32)
    wring = [sb(f"wring{i}", [128, SLOTW], BF16) for i in range(NSLOT)]
    hb = sb("hb", [128, D], BF16)
    junk = sb("junk", [128, D], BF16)
    gbc = sb("gbc", [128, D], F32)
    idf = sb("idf", [128, 128], F32)
    idb = sb("idb", [128, 128], BF16)
    pm_b = sb("pm_b", [128, 20, 128], BF16)
    flag = sb("flagt", [128, 1], F32)
    poolw_b = sb("poolw_b", [128, 2, 4, 128], BF16)
    wmT = sb("wmT", [128, 2, 2, 6, 128], BF16)
    brow = sb("brow", [1, 2, 6, 128], BF16)
    ones_row = sb("ones_row", [1, 128], BF16)
    convwT = sb("convwT", [128, 2, 6, 31], F32)
    vecT = sb("vecT", [128, 2, 6, 4], F32)
    gT = sb("gT", [128, 8, 16], F32)
    vbc = sb("vbc", [128, 768], F32)
    xa_prev = sb("xa_prev", [128, 2, 512], BF16)
    cext = sb("cext", [128, 2, 6, 30], F32)
    sp_b = sb("sp_b", [128, 2, 512], BF16)
    stat = sb("stat", [128, 64], F32)
    epsc = sb("epsc", [128, 2], F32)
    ps = nc.alloc_psum_tensor("ps", [128, 8, 512], F32)

    hT = blkB[:, 0:4096].bitcast(BF16).rearrange("p (k n) -> p k n", k=16)
    mixT = blkB[:, 4096:8192].bitcast(BF16).rearrange("p (k n) -> p k n", k=16)
    f_fm = blkB[:, 0:8192].rearrange("p (k n) -> p k n", k=16)
    hid = blkA[:, 0:11264].bitcast(BF16).rearrange("p (k n) -> p k n", k=44)
    m_fm = blkA[:, 0:8192].rearrange("p (k n) -> p k n", k=16)
    o = 0
    ext = blkA[:, o:o + 3252].rearrange("p (c n) -> p c n", c=6); o += 3252
    ext_s = blkA[:, o:o + 1488].rearrange("p (c b n) -> p c b n", c=6, b=4); o += 1488
    ycv = blkA[:, o:o + 3072].rearrange("p (c n) -> p c n", c=6); o += 3072
    xa_b = blkA[:, o:o + 1024].bitcast(BF16).rearrange("p (i n) -> p i n", i=4); o += 1024
    v_b = blkA[:, o:o + 1536].bitcast(BF16).rearrange("p (i n) -> p i n", i=4); o += 1536
    xa_f = blkA[:, o:o + 512]; o += 512
    v_f = blkA[:, o:o + 768]; o += 768
    hs_tok = blkA[:, o:o + 768]; o += 768
    u_sb = blkA[:, o:o + 512]; o += 512
    sg = blkA[:, o:o + 512]; o += 512
    yn_b = blkA[:, o:o + 384].bitcast(BF16); o += 384
    pg_b = blkA[:, o:o + 256].bitcast(BF16); o += 256
    sc_nat = blkA[:, o:o + 768]; o += 768
    xa_f2 = blkA[:, o:o + 512]; o += 512
    assert o <= A_WORDS, o
    hs_fm = gbc[:, 0:768].rearrange("p (c n) -> p c n", c=6)
    pm_f = blkA[:, 0:2560].rearrange("p (a n) -> p a n", a=20)
    ws_n = blkA[:, 2560:2560 + 768].rearrange("p (h n) -> p h n", h=6)
    ws_s = blkA[:, 3328:3328 + 768].rearrange("p (h n) -> p h n", h=6)
    pw_f = blkA[:, 4096:4096 + 512].rearrange("p (g n) -> p g n", g=4)
    cw_n = blkA[:, 4608:4608 + 768]
    vc_n = blkA[:, 5376:5376 + 768]
    g_n = blkA[:, 6144:6144 + 128]
    sb_f = blkA[:, 6272:6272 + 1536].rearrange("p (l h n) -> p l h n", l=2, h=6)
    sp_f = blkA[:, 7808:7808 + 1024].rearrange("p (l n) -> p l n", l=2)

    stg = ps[:, 4:8, :].rearrange("p a b -> p (a b)")

    def psb(b):
        return ps[:, b, :].bitcast(BF16)

    stack = ExitStack()
    S = Sched(nc, stack)
    accstate = {"i": 0}

    def acc():
        b = accstate["i"]
        accstate["i"] = (b + 1) % 4
        return b

    def acc2():
        b = 0 if accstate["i"] in (0, 3) else 2
        accstate["i"] = (b + 2) % 4
        return b

    PSK = lambda b: ("ps", b)
    STGK = [("ps", 4), ("ps", 5), ("ps", 6), ("ps", 7)]
    stat_col = {"i": 0}

    def scol(n=1):
        c = stat_col["i"]
        if c + n > 64:
            c = 0
        stat_col["i"] = c + n
        return c

    def dma1(out_ap, in_ap, **kw):
        return lambda e: [e.dma_start(out=out_ap, in_=in_ap, **kw)]

    S.op("sp", dma1(idf[:], ident_d[:, :]), writes=["idf"], dma=1)
    S.op("sp", dma1(pm_f, pmat_d[:, :, :]), writes=["pm_f"], dma=1)
    S.op("sp", dma1(flag[:], flag_d[:, :]), writes=["flag"], dma=1)
    S.op("dve", lambda e: e.tensor_copy(out=idb[:], in_=idf[:]), reads=["idf"], writes=["idb"])
    S.op("dve", lambda e: e.tensor_copy(out=pm_b[:], in_=pm_f), reads=["pm_f"], writes=["pm_b"])
    S.op("dve", lambda e: e.memset(ones_row[:], 1.0), writes=["ones_row"])
    S.op("dve", lambda e: e.memset(epsc[:, 0:1], RMS_EPS), writes=["epsc0"])
    S.op("dve", lambda e: e.memset(epsc[:, 1:2], LN_EPS), writes=["epsc1"])
    S.op("dve", lambda e: e.memset(cext[:], 0.0), writes=[("cext", 0), ("cext", 1)])
    S.op("dve", lambda e: e.memset(xa_prev[:], 0.0), writes=[("xa_prev", 0), ("xa_prev", 1)])
    S.op("sp", dma1(g_n, g4_d.rearrange("n (kc p) -> (n kc) p", p=128)), writes=["g_n"], dma=1)
    b = acc()
    S.op("pe", lambda e, b=b: e.transpose(ps[:, b, 0:128], g_n, idf[:]), reads=["g_n", "idf"], writes=[PSK(b)])
    S.op("dve", lambda e, b=b: e.tensor_copy(out=gT[:].rearrange("p a b -> p (a b)"), in_=ps[:, b, 0:128]),
         reads=[PSK(b)], writes=["gT"])
    S.op("sp", dma1(sb_f[0:1], sgu_b_d.rearrange("(o l) h n -> o l h n", o=1)), writes=["sb_f"], dma=1)
    S.op("sp", dma1(sp_f[0:60], spool_d.rearrange("l r n -> r l n")), writes=["sp_f"], dma=1)
    S.op("dve", lambda e: e.memset(sp_b[:], 0.0), writes=["sp_b"])
    S.op("dve", lambda e: e.tensor_copy(out=sp_b[0:60], in_=sp_f[0:60]), reads=["sp_f", "sp_b"], writes=["sp_b"])
    for l in range(DEPTH):
        S.op("sp", dma1(pw_f, pool_w_d[l].rearrange("g c d -> c g d")), writes=["pw_f"], dma=1)
        S.op("dve", lambda e, l=l: e.tensor_copy(out=poolw_b[:, l], in_=pw_f), reads=["pw_f"], writes=[("poolw", l)])
        S.op("sp", dma1(ws_n, sgu_ws_d[l].rearrange("h i j -> i h j")), writes=["ws_n"], dma=1)
        S.op("dve", lambda e: e.memset(ws_n[0:64, :, 64:128], 0.0), reads=[], writes=["ws_n"])
        S.op("dve", lambda e: e.memset(ws_s, 0.0), writes=["ws_s"])

        def ld_ws_s(e, l=l):
            r = []
            for bb in range(4):
                r.append(e.dma_start(out=ws_s[32 * bb:32 * bb + 32, :, 32 * bb:32 * bb + 32],
                                     in_=sgu_ws_d[l, :, 0:32, 0:32].rearrange("h i j -> i h j")))
            return r
        S.op("sp", ld_ws_s, writes=["ws_s"], dma=4)
        for kind, src, key in ((0, ws_n, "ws_n"), (1, ws_s, "ws_s")):
            b0 = acc2()

            def tr6(e, src=src, b0=b0):
                for h in range(6):
                    r = e.transpose(ps[:, b0 + h // 4, (h % 4) * 128:(h % 4 + 1) * 128], src[:, h, :], idf[:])
                return r
            S.op("pe", tr6, reads=[key, "idf"], writes=[PSK(b0), PSK(b0 + 1)])
            S.op("dve", lambda e, l=l, kind=kind, b0=b0: e.tensor_copy(
                out=wmT[:, l, kind].rearrange("p h n -> p (h n)"),
                in_=ps[:, b0:b0 + 2, :].rearrange("p a b -> p (a b)")[:, 0:768]),
                reads=[PSK(b0), PSK(b0 + 1)], writes=[("wmT", l, kind)])
        S.op("dve", lambda e, l=l: e.tensor_copy(out=brow[0:1, l], in_=sb_f[0:1, l]), reads=["sb_f"], writes=[("brow", l)])
        S.op("sp", dma1(cw_n[0:31], conv_w_d[l]), writes=["cw_n"], dma=1)
        b = acc()

        def trcw(e, b=b):
            for c in range(6):
                r = e.transpose(ps[:, b, c * 32:c * 32 + 31], cw_n[0:31, c * 128:(c + 1) * 128], idf[0:31, 0:31])
            return r
        S.op("pe", trcw, reads=["cw_n", "idf"], writes=[PSK(b)])
        S.op("dve", lambda e, l=l, b=b: e.tensor_copy(out=convwT[:, l], in_=ps[:, b, 0:192].rearrange("p (c k) -> p c k", c=6)[:, :, 0:31]),
             reads=[PSK(b)], writes=[("convwT", l)])
        S.op("sp", dma1(vc_n[0:4], vecs_d[l]), writes=["vc_n"], dma=1)
        b = acc()

        def trvc(e, b=b):
            for c in range(6):
                r = e.transpose(ps[:, b, c * 4:c * 4 + 4], vc_n[0:4, c * 128:(c + 1) * 128], idf[0:4, 0:4])
            return r
        S.op("pe", trvc, reads=["vc_n", "idf"], writes=[PSK(b)])
        S.op("dve", lambda e, l=l, b=b: e.tensor_copy(out=vecT[:, l].rearrange("p c k -> p (c k)"), in_=ps[:, b, 0:24]),
             reads=[PSK(b)], writes=[("vecT", l)])

    SETUP_KEYS = ["pm_f", "ws_n", "ws_s", "pw_f", "cw_n", "vc_n", "g_n", "sb_f", "sp_f"]

    wstate = {"i": 0}

    def wload(parts):
        s = wstate["i"]
        wstate["i"] = (s + 1) % NSLOT
        slot = wring[s]

        def fn(e, slot=slot, parts=parts):
            return [e.dma_start(out=dv(slot), in_=src) for dv, src in parts]
        S.op("pool", fn, writes=[("w", s)], dma=len(parts), wslot=s)
        return slot, ("w", s)

    def wtile_cols(wd_l, c0, w):
        src = wd_l.rearrange("(kc p) n -> p kc n", p=128)[:, :, c0:c0 + w]
        return [(lambda slot, w=w: slot[:, 0:16 * w].rearrange("p (kc n) -> p kc n", kc=16), src)]

    def slot_view(slot, w):
        return slot[:, 0:16 * w].rearrange("p (kc n) -> p kc n", kc=16)

    def rstd_chain(ssq_key, ssq_ap, n, eps_col, out_ap, out_key):
        S.op("act", lambda e: e.activation(out=out_ap, in_=ssq_ap, func=AF.Sqrt, bias=epsc[:, eps_col:eps_col + 1], scale=1.0 / n),
             reads=[ssq_key, "epsc0", "epsc1"], writes=[out_key], selfsync=True)
        S.op("dve", lambda e: e.reciprocal(out=out_ap, in_=out_ap), reads=[out_key], writes=[out_key])

    def prenorm_to_hT(tiles, gidx):
        for i, t in enumerate(tiles):
            c = scol(2)
            kq, kr = ("st", c), ("st", c + 1)
            S.op("act", lambda e, i=i, c=c: e.activation(out=junk[:], in_=xbuf[:, i, :], func=AF.Square, accum_out=stat[:, c:c + 1]),
                 reads=[("x", i)], writes=["junk", kq])
            rstd_chain(kq, stat[:, c:c + 1], D, 0, stat[:, c + 1:c + 2], kr)
            S.op("act", lambda e, i=i, c=c: e.activation(out=hb[:], in_=xbuf[:, i, :], func=AF.Identity, scale=stat[:, c + 1:c + 2]),
                 reads=[("x", i), kr], writes=["hb"])
            for half in range(2):
                b = acc()

                def tr8(e, half=half, b=b):
                    for q in range(8):
                        kc = half * 8 + q
                        r = e.transpose(psb(b)[:, q * 128:(q + 1) * 128], hb[:, kc * 128:(kc + 1) * 128], idb[:])
                    return r
                S.op("pe", tr8, reads=["hb", "idb"], writes=[PSK(b)])
                gsl = gT[:, gidx, half * 8:half * 8 + 8]
                S.op("dve", lambda e, i=i, half=half, b=b, gsl=gsl: e.tensor_tensor(
                    out=hT[:, half * 8:half * 8 + 8, i * 128:(i + 1) * 128],
                    in0=psb(b).rearrange("p (k n) -> p k n", k=8),
                    in1=gsl.unsqueeze(2).broadcast_to([128, 8, 128]), op=ALU.mult),
                    reads=[PSK(b), "gT"], writes=[("hT", i, half)])

    def postnorm_residual(tiles, src_fm, src_keyname, gidx, l):
        S.alias([("hs_fm", c) for c in range(6)], ["gbc"])
        S.op("sp", dma1(gbc[:], bass.AP(g4_d.tensor, gidx * D, [[0, 128], [1, D]])), writes=["gbc"], dma=1)
        for i, t in enumerate(tiles):
            def tr16(e, i=i):
                for j in range(16):
                    r = e.transpose(stg[:, j * 128:(j + 1) * 128], src_fm[:, j, i * 128:(i + 1) * 128], idf[:])
                return r
            S.op("pe", tr16, reads=[(src_keyname, j, i) for j in range(16)] + ["idf"], writes=STGK)
            c = scol(2)
            kq, kr = ("st", c), ("st", c + 1)
            S.op("act", lambda e, c=c: e.activation(out=junk[:], in_=stg, func=AF.Square, accum_out=stat[:, c:c + 1]),
                 reads=STGK, writes=["junk", kq])
            rstd_chain(kq, stat[:, c:c + 1], D, 0, stat[:, c + 1:c + 2], kr)
            tmpv = src_fm[:, :, i * 128:(i + 1) * 128]
            S.op("dve", lambda e, c=c, tmpv=tmpv: e.scalar_tensor_tensor(
                out=tmpv, in0=stg.rearrange("p (k n) -> p k n", k=16), scalar=stat[:, c + 1:c + 2],
                in1=gbc[:].rearrange("p (k n) -> p k n", k=16), op0=ALU.mult, op1=ALU.mult),
                reads=STGK + [kr, "gbc"], writes=[(src_keyname, j, i) for j in range(16)])
            S.op("dve", lambda e, i=i, tmpv=tmpv: e.tensor_tensor(
                out=xbuf[:, i, :].rearrange("p (k n) -> p k n", k=16), in0=xbuf[:, i, :].rearrange("p (k n) -> p k n", k=16),
                in1=tmpv, op=ALU.add),
                reads=[(src_keyname, j, i) for j in range(16)] + [("x", i)], writes=[("x", i)])

    MIXTMP_KEYS = ([("ext", c) for c in range(6)] + [("exts", c) for c in range(6)] + [("ycv", c) for c in range(6)] + [("ycvs", c) for c in range(6)]
                   + [("xa_b", i) for i in range(4)] + [("v_b", i) for i in range(4)]
                   + ["xa_f", "xa_f2", "v_f", "hs_tok", "u_sb", "sg", "yn_b", "pg_b", "sc_nat"])
    MFM_KEYS = [("mfm", j, i) for j in range(16) for i in range(4)]
    HID_KEYS = [("hid", j) for j in range(NFC)]
    FFM_KEYS = [("ffm", j, i) for j in range(16) for i in range(4)]
    BACT_KEYS = [("hT", i, h) for i in range(4) for h in range(2)] + [("mixT", c, i) for c in range(16) for i in range(4)]

    def layer(gi, tiles, l):
        nt = len(tiles)
        N = 128 * nt
        has_s = SAMPLE_T in tiles
        npt = nt - (1 if has_s else 0)
        Np = 128 * npt
        first = (gi == 0)
        w_in_l, w_out_l, w_gate_l, w_up_l, w_down_l = w_in_d[l], w_out_d[l], w_gate_d[l], w_up_d[l], w_down_d[l]
        hT_all = [("hT", i, h) for i in range(nt) for h in range(2)]

        S.new_phase()
        S.alias(FFM_KEYS, BACT_KEYS)
        S.alias(HID_KEYS + MFM_KEYS + SETUP_KEYS, MIXTMP_KEYS)
        S.alias(["gbc"], [("hs_fm", c) for c in range(6)])

        stage(1)
        S.op("sp", dma1(vbc[:], bass.AP(sgu_norm_d.tensor, l * 768, [[0, 128], [1, 768]])), writes=["vbc"], dma=1)
        prenorm_to_hT(tiles, l * 4 + 0)

        stage(2)
        sl0, k0 = wload(wtile_cols(w_in_l, 0, 256))
        sl1, k1 = wload(wtile_cols(w_in_l, 256, 256))
        for i, t in enumerate(tiles):
            b = acc()

            def mm_xa(e, i=i, b=b):
                for half, sl in ((0, sl0), (1, sl1)):
                    sv = slot_view(sl, 256)
                    for kc in range(16):
                        r = e.matmul(ps[:, b, half * 256:(half + 1) * 256], hT[:, kc, i * 128:(i + 1) * 128], sv[:, kc, :],
                                     start=(kc == 0), stop=(kc == 15))
                return r
            S.op("pe", mm_xa, reads=[("hT", i, 0), ("hT", i, 1), k0, k1], writes=[PSK(b)])
            if t == HALO_T:
                S.op("dve", lambda e, i=i, b=b: e.tensor_scalar(out=xa_b[:, i, :], in0=ps[:, b, :], scalar1=flag[:, 0:1], scalar2=None, op0=ALU.mult),
                     reads=[PSK(b), "flag"], writes=[("xa_b", i)])
            else:
                S.op("dve", lambda e, i=i, b=b: e.tensor_copy(out=xa_b[:, i, :], in_=ps[:, b, :]), reads=[PSK(b)], writes=[("xa_b", i)])
            if t in (LASTP_T, SAMPLE_T) and not ('noxaf16' in _DBG and t == LASTP_T) and not ('noxaf17' in _DBG and t == SAMPLE_T):
                xaf, xak = (xa_f, "xa_f") if t == LASTP_T else (xa_f2, "xa_f2")
                S.op("dve", lambda e, b=b, xaf=xaf: e.tensor_copy(out=xaf, in_=ps[:, b, :]), reads=[PSK(b)], writes=[xak])
                if 'nopoolst' not in _DBG:
                    S.op("sp", dma1(npool_p_d[l] if t == LASTP_T else npool_s_d[l], xaf), reads=[xak], dma=1)
            if t == SAMPLE_T:
                cur, prv, K2 = 12, 16, 128
                prev_ap = lambda g: sp_b[:, l, g * 128:(g + 1) * 128]
                prev_key = "sp_b"
                if 'genmat' in _DBG:
                    cur, prv = 0, 4
                if 'noprevs' in _DBG:
                    prev_ap = lambda g: xa_prev[:, l, g * 128:(g + 1) * 128]
                    prev_key = ("xa_prev", l)
            else:
                cur = 8 if t == 1 else 0
                prv, K2 = 4, 128
                if i == 0:
                    prev_ap = lambda g: xa_prev[:, l, g * 128:(g + 1) * 128]
                    prev_key = ("xa_prev", l)
                else:
                    prev_ap = lambda g, i=i: xa_b[:, i - 1, g * 128:(g + 1) * 128]
                    prev_key = ("xa_b", i - 1)
            b2 = acc()

            def mm_pool1(e, i=i, b2=b2, cur=cur, prv=prv, K2=K2, prev_ap=prev_ap):
                for g in range(4):
                    e.matmul(ps[:, b2, g * 128:(g + 1) * 128], xa_b[:, i, g * 128:(g + 1) * 128], pm_b[:, cur + g, :], start=True, stop=False)
                    r = e.matmul(ps[:, b2, g * 128:(g + 1) * 128], prev_ap(g), pm_b[0:K2, prv + g, :], start=False, stop=True)
                return r
            S.op("pe", mm_pool1, reads=[("xa_b", i), prev_key, "pm_b"], writes=[PSK(b2)])
            S.op("dve", lambda e, b2=b2: e.tensor_copy(out=pg_b, in_=ps[:, b2, :]), reads=[PSK(b2)], writes=["pg_b"])
            b3 = acc()

            def mm_pool2(e, b3=b3):
                for g in range(4):
                    r = e.matmul(ps[:, b3, g * 128:(g + 1) * 128], poolw_b[:, l, g, :], pg_b[:, g * 128:(g + 1) * 128], start=True, stop=True)
                return r
            S.op("pe", mm_pool2, reads=["pg_b", ("poolw", l)], writes=[PSK(b3)])

            def ev_pool(e, i=i, b3=b3):
                for g in range(4):
                    r = e.activation(out=mixT[:, g, i * 128:(i + 1) * 128], in_=ps[:, b3, g * 128:(g + 1) * 128], func=AF.Identity,
                                     scale=vecT[:, l, g, 0:1])
                return r
            S.op("act", ev_pool, reads=[PSK(b3), ("vecT", l)], writes=[("mixT", g, i) for g in range(4)])
        S.op("dve", lambda e: e.tensor_copy(out=xa_prev[:, l, :], in_=xa_b[:, npt - 1, :]), reads=[("xa_b", npt - 1)], writes=[("xa_prev", l)])

        stage(3)
        slv = [wload(wtile_cols(w_in_l, 1280 + 256 * q, 256)) for q in range(3)]
        for i, t in enumerate(tiles):
            b0 = acc2()
            pv = ps[:, b0:b0 + 2, :].rearrange("p a b -> p (a b)")

            def mm_v(e, i=i, pv=pv):
                for q in range(3):
                    sv = slot_view(slv[q][0], 256)
                    for kc in range(16):
                        r = e.matmul(pv[:, q * 256:(q + 1) * 256], hT[:, kc, i * 128:(i + 1) * 128], sv[:, kc, :], start=(kc == 0), stop=(kc == 15))
                return r
            S.op("pe", mm_v, reads=[("hT", i, 0), ("hT", i, 1)] + [k for _, k in slv], writes=[PSK(b0), PSK(b0 + 1)])
            c = scol(2)
            kq, kr = ("st", c), ("st", c + 1)
            S.op("act", lambda e, pv=pv, c=c: e.activation(out=junk[:, 0:768], in_=pv[:, 0:768], func=AF.Square, accum_out=stat[:, c:c + 1]),
                 reads=[PSK(b0), PSK(b0 + 1)], writes=["junk", kq])
            rstd_chain(kq, stat[:, c:c + 1], 768, 0, stat[:, c + 1:c + 2], kr)
            if t == SAMPLE_T:
                S.op("dve", lambda e, pv=pv, c=c: e.scalar_tensor_tensor(out=v_f, in0=pv[:, 0:768], scalar=stat[:, c + 1:c + 2], in1=vbc[:],
                                                                         op0=ALU.mult, op1=ALU.mult),
                     reads=[PSK(b0), PSK(b0 + 1), kr, "vbc"], writes=["v_f"])
                S.op("dve", lambda e, i=i: e.tensor_copy(out=v_b[:, i, :], in_=v_f), reads=["v_f"], writes=[("v_b", i)])
                S.op("sp", dma1(nv_s_d[l], v_f), reads=["v_f"], dma=1)
            else:
                S.op("dve", lambda e, pv=pv, c=c, i=i: e.scalar_tensor_tensor(out=v_b[:, i, :], in0=pv[:, 0:768], scalar=stat[:, c + 1:c + 2],
                                                                              in1=vbc[:], op0=ALU.mult, op1=ALU.mult),
                     reads=[PSK(b0), PSK(b0 + 1), kr, "vbc"], writes=[("v_b", i)])

        stage(4)
        slu = [wload(wtile_cols(w_in_l, 512 + 256 * q, 256)) for q in range(3)]
        for h in range(6):
            slot, wk = slu[h // 2]
            c0 = (h % 2) * 128
            bU = acc()

            def mm_u(e, slot=slot, c0=c0, bU=bU):
                sv = slot_view(slot, 256)
                for kc in range(16):
                    r = e.matmul(ps[:, bU, 0:N], sv[:, kc, c0:c0 + 128], hT[:, kc, 0:N], start=(kc == 0), stop=(kc == 15))
                return r
            S.op("pe", mm_u, reads=hT_all + [wk], writes=[PSK(bU)])
            bM = acc()

            def mm_sgu(e, h=h, bM=bM):
                for i, t in enumerate(tiles):
                    if t == SAMPLE_T:
                        e.matmul(ps[:, bM, i * 128:(i + 1) * 128], v_b[:, i, h * 128:(h + 1) * 128], wmT[:, l, 1, h, :], start=True, stop=False)
                        for bb in range(4):
                            r = e.matmul(ps[:, bM, i * 128 + 32 * bb:i * 128 + 32 * bb + 32], ones_row[0:1, :], brow[0:1, l, h, 0:32],
                                         start=False, stop=(bb == 3))
                    else:
                        e.matmul(ps[:, bM, i * 128:(i + 1) * 128], v_b[:, i, h * 128:(h + 1) * 128], wmT[:, l, 0, h, :], start=True, stop=False)
                        r = e.matmul(ps[:, bM, i * 128:(i + 1) * 128], ones_row[0:1, :], brow[0:1, l, h, :], start=False, stop=True)
                return r
            S.op("pe", mm_sgu, reads=[("v_b", i) for i in range(nt)] + [("wmT", l, 0), ("wmT", l, 1), ("brow", l), "ones_row"], writes=[PSK(bM)])
            S.op("act", lambda e, bU=bU: e.activation(out=u_sb[:, 0:N], in_=ps[:, bU, 0:N], func=AF.Copy), reads=[PSK(bU)], writes=["u_sb"])
            S.op("dve", lambda e, h=h, bM=bM: e.tensor_tensor(out=mixT[:, 4 + h, 0:N], in0=u_sb[:, 0:N], in1=ps[:, bM, 0:N], op=ALU.mult),
                 reads=["u_sb", PSK(bM)], writes=[("mixT", 4 + h, i) for i in range(nt)])

        stage(5)
        S.op("dve", lambda e: e.tensor_copy(out=ext[:, :, 0:30], in_=cext[:, l]), reads=[("cext", l)], writes=[("ext", c) for c in range(6)])
        if has_s:
            S.op("sp", dma1(sc_nat[0:120], sconv_d[l]), writes=["sc_nat"], dma=1)
            for c in range(6):
                b = acc()
                S.op("pe", lambda e, c=c, b=b: e.transpose(ps[:, b, 0:120], sc_nat[0:120, c * 128:(c + 1) * 128], idf[0:120, 0:120]),
                     reads=["sc_nat", "idf"], writes=[PSK(b)])
                S.op("dve", lambda e, c=c, b=b: e.tensor_copy(out=ext_s[:, c, :, 0:30], in_=ps[:, b, 0:120].rearrange("p (b r) -> p b r", b=4)),
                     reads=[PSK(b)], writes=[("exts", c)])
        for c in range(6):
            slot, wk = wload([
                (lambda s: slot_view(s, 256)[:, :, 0:128], w_in_l.rearrange("(kc p) n -> p kc n", p=128)[:, :, 2048 + 128 * c:2048 + 128 * c + 128]),
                (lambda s: slot_view(s, 256)[:, :, 128:256], w_in_l.rearrange("(kc p) n -> p kc n", p=128)[:, :, 2816 + 128 * c:2816 + 128 * c + 128]),
            ])
            bA, bG = acc(), acc()

            def mm_cc(e, slot=slot, bA=bA, bG=bG):
                sv = slot_view(slot, 256)
                for bb, c0 in ((bA, 0), (bG, 128)):
                    for kc in range(16):
                        r = e.matmul(ps[:, bb, 0:N], sv[:, kc, c0:c0 + 128], hT[:, kc, 0:N], start=(kc == 0), stop=(kc == 15))
                return r
            S.op("pe", mm_cc, reads=hT_all + [wk], writes=[PSK(bA), PSK(bG)])
            S.op("act", lambda e, bG=bG: e.activation(out=sg[:, 0:N], in_=ps[:, bG, 0:N], func=AF.Sigmoid), reads=[PSK(bG)], writes=["sg"])
            S.op("dve", lambda e, c=c, bA=bA: e.tensor_tensor(out=ext[:, c, 30:30 + Np], in0=ps[:, bA, 0:Np], in1=sg[:, 0:Np], op=ALU.mult),
                 reads=[PSK(bA), "sg"], writes=[("ext", c)])
            if has_s:
                S.op("dve", lambda e, c=c, bA=bA: e.tensor_tensor(out=hs_fm[:, c, :], in0=ps[:, bA, Np:N], in1=sg[:, Np:N], op=ALU.mult),
                     reads=[PSK(bA), "sg"], writes=[("hs_fm", c)])
                S.op("dve", lambda e, c=c: e.tensor_copy(out=ext_s[:, c, :, 30:62], in_=hs_fm[:, c, :].rearrange("p (b j) -> p b j", b=4)),
                     reads=[("hs_fm", c)], writes=[("exts", c)])
            if first:
                S.op("dve", lambda e, c=c: e.tensor_scalar(out=ext[:, c, 30:158], in0=ext[:, c, 30:158], scalar1=flag[:, 0:1], scalar2=None, op0=ALU.mult),
                     reads=[("ext", c), "flag"], writes=[("ext", c)])

            ys_ = ycv[:, c, Np:N].rearrange("p (b j) -> p b j", b=4) if has_s else None
            for k in range(31):
                if k == 0:
                    S.op("dve", lambda e, c=c: e.tensor_scalar(out=ycv[:, c, 0:Np], in0=ext[:, c, 0:Np], scalar1=convwT[:, l, c, 0:1],
                                                               scalar2=vecT[:, l, c, 1:2], op0=ALU.mult, op1=ALU.add),
                         reads=[("ext", c), ("convwT", l), ("vecT", l)], writes=[("ycv", c)])
                    if has_s:
                        S.op("dve", lambda e, c=c, ys_=ys_: e.tensor_scalar(out=ys_, in0=ext_s[:, c, :, 0:32], scalar1=convwT[:, l, c, 0:1],
                                                                            scalar2=vecT[:, l, c, 1:2], op0=ALU.mult, op1=ALU.add),
                             reads=[("exts", c), ("convwT", l), ("vecT", l)], writes=[("ycvs", c)])
                else:
                    S.op("dve", lambda e, c=c, k=k: e.scalar_tensor_tensor(out=ycv[:, c, 0:Np], in0=ext[:, c, k:k + Np], scalar=convwT[:, l, c, k:k + 1],
                                                                           in1=ycv[:, c, 0:Np], op0=ALU.mult, op1=ALU.add),
                         reads=[("ext", c), ("ycv", c)], writes=[("ycv", c)])
                    if has_s:
                        S.op("dve", lambda e, c=c, k=k, ys_=ys_: e.scalar_tensor_tensor(out=ys_, in0=ext_s[:, c, :, k:k + 32], scalar=convwT[:, l, c, k:k + 1],
                                                                                        in1=ys_, op0=ALU.mult, op1=ALU.add),
                             reads=[("exts", c), ("ycvs", c)], writes=[("ycvs", c)])
        S.op("dve", lambda e: e.tensor_copy(out=cext[:, l], in_=ext[:, :, Np:Np + 30]), reads=[("ext", c) for c in range(6)], writes=[("cext", l)])
        if LASTP_T in tiles:
            ip = tiles.index(LASTP_T)

            def tr_hp(e, ip=ip):
                for c in range(6):
                    r = e.transpose(stg[:, c * 128:(c + 1) * 128], ext[:, c, 30 + ip * 128:30 + (ip + 1) * 128], idf[:])
                return r
            S.op("pe", tr_hp, reads=[("ext", c) for c in range(6)] + ["idf"], writes=STGK[0:2])
            S.op("dve", lambda e: e.tensor_copy(out=hs_tok, in_=stg[:, 0:768]), reads=STGK[0:2], writes=["hs_tok"])
            S.op("sp", dma1(nconv_p_d[l], hs_tok), reads=["hs_tok"], dma=1)
        if has_s:
            def tr_hs(e):
                for c in range(6):
                    r = e.transpose(stg[:, c * 128:(c + 1) * 128], hs_fm[:, c, :], idf[:])
                return r
            S.op("pe", tr_hs, reads=[("hs_fm", c) for c in range(6)] + ["idf"], writes=STGK[0:2])
            S.op("dve", lambda e: e.tensor_copy(out=hs_tok, in_=stg[:, 0:768]), reads=STGK[0:2], writes=["hs_tok"])

            S.op("sp", dma1(nconv_s_d[l], hs_tok), reads=["hs_tok"], dma=1)

        stage(6)
        for i, t in enumerate(tiles):
            def tr_y(e, i=i):
                for c in range(6):
                    r = e.transpose(stg[:, c * 128:(c + 1) * 128], ycv[:, c, i * 128:(i + 1) * 128], idf[:])
                return r
            S.op("pe", tr_y, reads=[("ycv", c) for c in range(6)] + [("ycvs", c) for c in range(6)] + ["idf"], writes=STGK[0:2])
            c0 = scol(6)
            k1_, k2_ = ("st", c0), ("st", c0 + 1)
            S.op("act", lambda e, c0=c0: e.activation(out=junk[:, 0:768], in_=stg[:, 0:768], func=AF.Identity, accum_out=stat[:, c0:c0 + 1]),
                 reads=STGK[0:2], writes=["junk", k1_])
            S.op("act", lambda e, c0=c0: e.activation(out=junk[:, 0:768], in_=stg[:, 0:768], func=AF.Square, accum_out=stat[:, c0 + 1:c0 + 2]),
                 reads=STGK[0:2], writes=["junk", k2_])
            km, kv, kr, kn = ("st", c0 + 2), ("st", c0 + 3), ("st", c0 + 4), ("st", c0 + 5)
            S.op("dve", lambda e, c0=c0: e.tensor_scalar(out=stat[:, c0 + 2:c0 + 3], in0=stat[:, c0:c0 + 1], scalar1=1.0 / 768, scalar2=None, op0=ALU.mult),
                 reads=[k1_], writes=[km])
            S.op("dve", lambda e, c0=c0: e.scalar_tensor_tensor(out=stat[:, c0 + 3:c0 + 4], in0=stat[:, c0 + 2:c0 + 3], scalar=-1.0, in1=stat[:, c0 + 2:c0 + 3],
                                                                op0=ALU.mult, op1=ALU.mult),
                 reads=[km], writes=[kv], selfsync=True)
            S.op("dve", lambda e, c0=c0: e.scalar_tensor_tensor(out=stat[:, c0 + 3:c0 + 4], in0=stat[:, c0 + 1:c0 + 2], scalar=1.0 / 768, in1=stat[:, c0 + 3:c0 + 4],
                                                                op0=ALU.mult, op1=ALU.add),
                 reads=[k2_, kv], writes=[kv], selfsync=True)
            S.op("act", lambda e, c0=c0: e.activation(out=stat[:, c0 + 4:c0 + 5], in_=stat[:, c0 + 3:c0 + 4], func=AF.Sqrt, bias=epsc[:, 1:2], scale=1.0),
                 reads=[kv, "epsc1"], writes=[kr])
            S.op("dve", lambda e, c0=c0: e.reciprocal(out=stat[:, c0 + 4:c0 + 5], in_=stat[:, c0 + 4:c0 + 5]), reads=[kr], writes=[kr])
            S.op("dve", lambda e, c0=c0: e.scalar_tensor_tensor(out=stat[:, c0 + 5:c0 + 6], in0=stat[:, c0 + 2:c0 + 3], scalar=-1.0, in1=stat[:, c0 + 4:c0 + 5],
                                                                op0=ALU.mult, op1=ALU.mult),
                 reads=[km, kr], writes=[kn], selfsync=True)
            S.op("dve", lambda e, c0=c0: e.tensor_scalar(out=yn_b, in0=stg[:, 0:768], scalar1=stat[:, c0 + 4:c0 + 5], scalar2=stat[:, c0 + 5:c0 + 6],
                                                         op0=ALU.mult, op1=ALU.add),
                 reads=STGK[0:2] + [kr, kn], writes=["yn_b"], selfsync=True)
            b = acc()

            def tr_yn(e, b=b):
                for c in range(6):
                    r = e.transpose(psb(b)[:, c * 128:(c + 1) * 128], yn_b[:, c * 128:(c + 1) * 128], idb[:])
                return r
            S.op("pe", tr_yn, reads=["yn_b", "idb"], writes=[PSK(b)])

            def ev_yc(e, i=i, b=b):
                for c in range(6):
                    r = e.activation(out=mixT[:, 10 + c, i * 128:(i + 1) * 128], in_=psb(b)[:, c * 128:(c + 1) * 128], func=AF.Silu,
                                     bias=vecT[:, l, c, 3:4], scale=vecT[:, l, c, 2:3])
                return r
            S.op("act", ev_yc, reads=[PSK(b), ("vecT", l)], writes=[("mixT", 10 + c, i) for c in range(6)])

        stage(7)
        S.alias(MIXTMP_KEYS + [("hs_fm", c) for c in range(6)], MFM_KEYS)
        mixT_all = [("mixT", c, i) for c in range(16) for i in range(nt)]
        for jb in range(8):
            slot, wk = wload(wtile_cols(w_out_l, 256 * jb, 256))
            for cc in range(2):
                j = 2 * jb + cc
                b = acc()

                def mm_o(e, slot=slot, cc=cc, b=b):
                    sv = slot_view(slot, 256)
                    for kc in range(16):
                        r = e.matmul(ps[:, b, 0:N], sv[:, kc, cc * 128:(cc + 1) * 128], mixT[:, kc, 0:N], start=(kc == 0), stop=(kc == 15))
                    return r
                S.op("pe", mm_o, reads=mixT_all + [wk], writes=[PSK(b)])
                if j % 2 == 0:
                    S.op("act", lambda e, j=j, b=b: e.activation(out=m_fm[:, j, 0:N], in_=ps[:, b, 0:N], func=AF.Copy),
                         reads=[PSK(b)], writes=[("mfm", j, i) for i in range(nt)])
                else:
                    S.op("dve", lambda e, j=j, b=b: e.tensor_copy(out=m_fm[:, j, 0:N], in_=ps[:, b, 0:N]),
                         reads=[PSK(b)], writes=[("mfm", j, i) for i in range(nt)])
        postnorm_residual(tiles, m_fm, "mfm", l * 4 + 1, l)

        stage(8)
        prenorm_to_hT(tiles, l * 4 + 2)

        stage(9)
        S.alias(MFM_KEYS + MIXTMP_KEYS, HID_KEYS)
        for jb in range(22):
            slg, kg = wload(wtile_cols(w_gate_l, 256 * jb, 256))
            slu_, ku = wload(wtile_cols(w_up_l, 256 * jb, 256))
            for cc in range(2):
                j = 2 * jb + cc
                bG, bU = acc(), acc()

                def mm_gu(e, slg=slg, slu_=slu_, cc=cc, bG=bG, bU=bU):
                    for bb, sl in ((bG, slg), (bU, slu_)):
                        sv = slot_view(sl, 256)
                        for kc in range(16):
                            r = e.matmul(ps[:, bb, 0:N], sv[:, kc, cc * 128:(cc + 1) * 128], hT[:, kc, 0:N], start=(kc == 0), stop=(kc == 15))
                    return r
                S.op("pe", mm_gu, reads=hT_all + [kg, ku], writes=[PSK(bG), PSK(bU)])
                S.op("act", lambda e, bG=bG: e.activation(out=sg[:, 0:N], in_=ps[:, bG, 0:N], func=AF.Silu), reads=[PSK(bG)], writes=["sg"])
                S.op("dve", lambda e, j=j, bU=bU: e.tensor_tensor(out=hid[:, j, 0:N], in0=sg[:, 0:N], in1=ps[:, bU, 0:N], op=ALU.mult),
                     reads=["sg", PSK(bU)], writes=[("hid", j)])

        stage(10)
        S.alias(BACT_KEYS, FFM_KEYS)
        hid_all = [("hid", j) for j in range(NFC)]
        wdv = w_down_l.rearrange("(fc p) n -> p fc n", p=128)
        for j in range(16):
            halves = []
            for hh in range(2):
                halves.append(wload([(lambda s: s[:, 0:22 * 128].rearrange("p (fc n) -> p fc n", fc=22),
                                      wdv[:, hh * 22:(hh + 1) * 22, j * 128:(j + 1) * 128])]))
            b = acc()

            def mm_d(e, halves=halves, b=b):
                for hh in range(2):
                    sv = halves[hh][0][:, 0:22 * 128].rearrange("p (fc n) -> p fc n", fc=22)
                    for f in range(22):
                        fc = hh * 22 + f
                        r = e.matmul(ps[:, b, 0:N], sv[:, f, :], hid[:, fc, 0:N], start=(fc == 0), stop=(fc == NFC - 1))
                return r
            S.op("pe", mm_d, reads=hid_all + [halves[0][1], halves[1][1]], writes=[PSK(b)])
            if j % 2 == 0:
                S.op("act", lambda e, j=j, b=b: e.activation(out=f_fm[:, j, 0:N], in_=ps[:, b, 0:N], func=AF.Copy),
                     reads=[PSK(b)], writes=[("ffm", j, i) for i in range(nt)])
            else:
                S.op("dve", lambda e, j=j, b=b: e.tensor_copy(out=f_fm[:, j, 0:N], in_=ps[:, b, 0:N]),
                     reads=[PSK(b)], writes=[("ffm", j, i) for i in range(nt)])
        postnorm_residual(tiles, f_fm, "ffm", l * 4 + 3, l)

    try:
        stage(0)
        for gi, tiles in enumerate(groups):
            for i, t in enumerate(tiles):
                S.op("sp", dma1(xbuf[:, i, :], xin[t]), writes=[("x", i)], dma=1)
            for l in range(DEPTH):
                layer(gi, tiles, l)
            for i, t in enumerate(tiles):
                if t == HALO_T:
                    continue
                dst = ys_d[:, :] if t == SAMPLE_T else yp_d[t - 1]
                S.op("sp", dma1(dst, xbuf[:, i, :]), reads=[("x", i)], dma=1)
    except _StopBuild:
        pass

    def fin(e):
        for s, v in S.all_dma.items():
            if s.name.startswith("dq"):
                e.wait_ge(s, v)
        return []
    S.ops["sp"].append(({}, fin, None, True))

    with stack:
        with nc.Block() as block:
            @block.sync
            def _(e):
                S.emit("sp", e)

            @block.gpsimd
            def _(e):
                S.emit("pool", e)

            @block.tensor
            def _(e):
                S.emit("pe", e)

            @block.scalar
            def _(e):
                S.emit("act", e)

            @block.vector
            def _(e):
                S.emit("dve", e)
    return nc


def _pool_mats(core):
    W = (2, 4, 8, 16)
    pm = np.zeros((20, 128, 128), np.float32)
    tp = np.arange(128)[:, None]
    t = np.arange(128)[None, :]
    for g, w in enumerate(W):
        cur = ((t - tp >= 0) & (t - tp < w)).astype(np.float32) / w - (t == tp)
        prev = ((t - (tp - 128)) < w).astype(np.float32) / w
        pm[g] = cur
        pm[4 + g] = prev
        if core == 0:
            cnt = np.minimum(t + 1, w).astype(np.float32)
            pm[8 + g] = ((t - tp >= 0) & (t - tp < w)).astype(np.float32) / cnt - (t == tp)
        else:
            pm[8 + g] = cur
        bi, ii = np.divmod(np.arange(128), 32)
        same = (bi[:, None] == bi[None, :])
        dj = ii[None, :] - ii[:, None]
        pm[12 + g] = (same & (dj >= 0) & (dj < w)).astype(np.float32) / w - np.eye(128, dtype=np.float32)
        hist = np.zeros((128, 128), np.float32)
        rb, rr = np.divmod(np.arange(60), 15)
        m = (rb[:, None] == bi[None, :]) & ((ii[None, :] - (rr[:, None] - 15)) < w)
        hist[:60] = m.astype(np.float32) / w
        pm[16 + g] = hist
    return np.ascontiguousarray(pm.transpose(1, 0, 2))


_NC_CACHE = {}


def kernel(x_prompt, x_sample, state_pool, state_conv, norm_pre_mix, norm_post_mix, norm_pre_ffn, norm_post_ffn,
           w_in, w_out, pool_w, pool_scale, sgu_norm, sgu_ws, sgu_b, conv_w, conv_b, conv_ln_g, conv_ln_b,
           w_gate, w_up, w_down):
    f = lambda a: np.ascontiguousarray(np.asarray(a, dtype=np.float32))
    x_prompt, x_sample, state_pool, state_conv = f(x_prompt), f(x_sample), f(state_pool), f(state_conv)
    xp = x_prompt[0]
    g4 = np.stack([f(norm_pre_mix), f(norm_post_mix), f(norm_pre_ffn), f(norm_post_ffn)], axis=1).reshape(8, D)
    vecs = np.zeros((DEPTH, 4, 768), np.float32)
    vecs[:, 0, :512] = f(pool_scale)
    vecs[:, 1] = f(conv_b)
    vecs[:, 2] = f(conv_ln_g)
    vecs[:, 3] = f(conv_ln_b)
    shared = {
        "ident": np.eye(128, dtype=np.float32),
        "w_in": f(w_in), "w_out": f(w_out), "w_gate": f(w_gate), "w_up": f(w_up), "w_down": f(w_down),
        "g4": np.ascontiguousarray(g4), "pool_w": f(pool_w), "vecs": vecs, "sgu_norm": f(sgu_norm),
        "sgu_ws": f(sgu_ws), "sgu_b": f(sgu_b), "conv_w": f(conv_w),
    }
    in_maps = []
    for c in range(NCORES):
        xin = np.zeros((NT, 128, D), np.float32)
        if c > 0:
            xin[0] = xp[c * 2048 - 128:c * 2048]
        xin[1:17] = xp[c * 2048:(c + 1) * 2048].reshape(16, 128, D)
        xin[17] = x_sample[4 * c:4 * c + 4].reshape(128, D)
        m = dict(shared)
        m["xin"] = xin
        m["flag"] = np.full((128, 1), 0.0 if c == 0 else 1.0, np.float32)
        m["spool"] = np.ascontiguousarray(state_pool[:, 4 * c:4 * c + 4].reshape(DEPTH, 60, 512))
        m["sconv"] = np.ascontiguousarray(state_conv[:, 4 * c:4 * c + 4].reshape(DEPTH, 120, 768))
        m["pmat"] = _pool_mats(c)
        in_maps.append(m)
    if "nc" not in _NC_CACHE:
        _NC_CACHE["nc"] = build_program()
    nc = _NC_CACHE["nc"]
    res = run_bass_kernel_spmd(nc, in_maps, core_ids=list(range(NCORES)))
    R = res.results
    y_prompt = np.concatenate([R[c]["yp"].reshape(2048, D) for c in range(NCORES)], axis=0)[None]
    y_sample = np.concatenate([R[c]["ys"].reshape(4, 32, D) for c in range(NCORES)], axis=0)
    new_pool_p = R[NCORES - 1]["npool_p"][:, None, 113:128]
    new_conv_p = R[NCORES - 1]["nconv_p"][:, None, 98:128]
    new_pool_s = np.concatenate([R[c]["npool_s"].reshape(DEPTH, 4, 32, 512)[:, :, 17:32] for c in range(NCORES)], axis=1)
    new_conv_s = np.concatenate([R[c]["nconv_s"].reshape(DEPTH, 4, 32, 768)[:, :, 2:32] for c in range(NCORES)], axis=1)
    new_v_s = np.concatenate([R[c]["nv_s"].reshape(DEPTH, 4, 32, 768) for c in range(NCORES)], axis=1)
    out = (y_prompt, y_sample, new_pool_p, new_conv_p, new_pool_s, new_conv_s, new_v_s)
    return tuple(np.ascontiguousarray(o, dtype=np.float32) for o in out)
```

```python
from contextlib import ExitStack
import os
_DBG = os.environ.get('KDBG', '')
import numpy as np
import concourse.bass as bass
import concourse.mybir as mybir
from concourse.bass_utils import run_bass_kernel_spmd

F32 = mybir.dt.float32
BF16 = mybir.dt.bfloat16
AF = mybir.ActivationFunctionType
ALU = mybir.AluOpType

NCORES = 8
D = 2048
INW = 3584
DFF = 5632
NKC = 16
NFC = 44
DEPTH = 2
NT = 18
HALO_T, LASTP_T, SAMPLE_T = 0, 16, 17
GROUPS = [[0, 1, 2, 3], [4, 5, 6, 7], [8, 9, 10, 11], [12, 13, 14, 15], [16, 17]]
RMS_EPS = 1e-6
LN_EPS = 1e-5
NSLOT = 3
SLOTW = 4096
NDSEM = 40


class Sched:
    def __init__(self, nc, stack):
        self.nc = nc
        self.stack = stack
        self.engs = ["pe", "act", "dve", "pool", "sp"]
        self.ops = {e: [] for e in self.engs}
        self.semc = 0
        self.msem = {}
        for e in ("pe", "act", "dve", "pool"):
            self._new_msem(e)
        self.dsems = [stack.enter_context(nc.semaphore(f"dq{i}")) for i in range(NDSEM)]
        self.dcnt = [0] * NDSEM
        self.drr = 0
        self.wsems = [stack.enter_context(nc.semaphore(f"wq{i}")) for i in range(NSLOT)]
        self.wcnt = [0] * NSLOT
        self.lastw = {}
        self.readers = {}
        self.all_dma = {}

    def _new_msem(self, e):
        s = self.stack.enter_context(self.nc.semaphore(f"m{e}{self.semc}"))
        self.semc += 1
        self.msem[e] = [s, 0]

    def new_phase(self):
        for e in ("pe", "act", "dve"):
            self._new_msem(e)

    @staticmethod
    def _add(d, ev):
        s, v = ev
        if d.get(s, 0) < v:
            d[s] = v

    def op(self, eng, fn, reads=(), writes=(), dma=0, wslot=None, selfsync=False):
        deps = {}
        for k in reads:
            ev = self.lastw.get(k)
            if ev is not None:
                self._add(deps, ev)
        for k in writes:
            ev = self.lastw.get(k)
            if ev is not None:
                self._add(deps, ev)
            for s, v in self.readers.get(k, {}).items():
                self._add(deps, (s, v))
        if dma:
            if wslot is not None:
                sem = self.wsems[wslot]
                self.wcnt[wslot] += 16 * dma
                val = self.wcnt[wslot]
            else:
                i = self.drr
                self.drr = (self.drr + 1) % NDSEM
                sem = self.dsems[i]
                if self.dcnt[i] > 0:
                    self._add(deps, (sem, self.dcnt[i]))
                self.dcnt[i] += 16 * dma
                val = self.dcnt[i]
            ev = (sem, val)
            self.all_dma[sem] = val
        else:
            ms = self.msem[eng]
            own = ms[0]
            if eng == "pe":
                for s in [s for s in deps if s.name.startswith("m" + eng)]:
                    del deps[s]
            ms[1] += 1
            ev = (own, ms[1])
        self.ops[eng].append((deps, fn, ev, bool(dma)))
        for k in writes:
            self.lastw[k] = ev
            self.readers[k] = {}
        for k in reads:
            self._add(self.readers.setdefault(k, {}), ev)
        return ev

    def alias(self, old_keys, new_keys):
        evs = {}
        for k in old_keys:
            ev = self.lastw.get(k)
            if ev is not None:
                self._add(evs, ev)
            for s, v in self.readers.get(k, {}).items():
                self._add(evs, (s, v))
        for k in new_keys:
            r = self.readers.setdefault(k, {})
            for s, v in evs.items():
                self._add(r, (s, v))

    def emit(self, eng_name, eng):
        seen = {}
        for deps, fn, ev, is_dma in self.ops[eng_name]:
            for s, v in deps.items():
                if seen.get(s, 0) < v:
                    eng.wait_ge(s, v)
                    seen[s] = v
            r = fn(eng)
            if is_dma:
                for ins in r:
                    ins.then_inc(ev[0], 16)
            else:
                r.then_inc(ev[0], 1)


class _StopBuild(Exception):
    pass


def build_program(groups=None, stop_after=None):
    groups = GROUPS if groups is None else groups

    def stage(k):
        if stop_after is not None and k > stop_after:
            raise _StopBuild()
    nc = bass.Bass("TRN2", target_bir_lowering=False)

    def din(name, shape):
        return nc.dram_tensor(name, list(shape), F32, kind="ExternalInput").ap()

    def dout(name, shape):
        return nc.dram_tensor(name, list(shape), F32, kind="ExternalOutput").ap()

    xin = din("xin", [NT, 128, D])
    flag_d = din("flag", [128, 1])
    spool_d = din("spool", [DEPTH, 60, 512])
    sconv_d = din("sconv", [DEPTH, 120, 768])
    pmat_d = din("pmat", [128, 20, 128])
    ident_d = din("ident", [128, 128])
    w_in_d = din("w_in", [DEPTH, D, INW])
    w_out_d = din("w_out", [DEPTH, D, D])
    w_gate_d = din("w_gate", [DEPTH, D, DFF])
    w_up_d = din("w_up", [DEPTH, D, DFF])
    w_down_d = din("w_down", [DEPTH, DFF, D])
    g4_d = din("g4", [8, D])
    pool_w_d = din("pool_w", [DEPTH, 4, 128, 128])
    vecs_d = din("vecs", [DEPTH, 4, 768])
    sgu_norm_d = din("sgu_norm", [DEPTH, 768])
    sgu_ws_d = din("sgu_ws", [DEPTH, 6, 128, 128])
    sgu_b_d = din("sgu_b", [DEPTH, 6, 128])
    conv_w_d = din("conv_w", [DEPTH, 31, 768])

    yp_d = dout("yp", [16, 128, D])
    ys_d = dout("ys", [128, D])
    npool_p_d = dout("npool_p", [DEPTH, 128, 512])
    nconv_p_d = dout("nconv_p", [DEPTH, 128, 768])
    npool_s_d = dout("npool_s", [DEPTH, 128, 512])
    nconv_s_d = dout("nconv_s", [DEPTH, 128, 768])
    nv_s_d = dout("nv_s", [DEPTH, 128, 768])

    def sb(name, shape, dt):
        return nc.alloc_sbuf_tensor(name, list(shape), dt)

    xbuf = sb("xbuf", [128, 4, D], F32)
    blkB = sb("blkB", [128, 8192], F32)
    A_WORDS = 14852 + 512
    blkA = sb("blkA", [128, A_WORDS], F32)
    wring = [sb(f"wring{i}", [128, SLOTW], BF16) for i in range(NSLOT)]
    hb = sb("hb", [128, D], BF16)
    junk = sb("junk", [128, D], BF16)
    gbc = sb("gbc", [128, D], F32)
    idf = sb("idf", [128, 128], F32)
    idb = sb("idb", [128, 128], BF16)
    pm_b = sb("pm_b", [128, 20, 128], BF16)
    flag = sb("flagt", [128, 1], F32)
    poolw_b = sb("poolw_b", [128, 2, 4, 128], BF16)
    wmT = sb("wmT", [128, 2, 2, 6, 128], BF16)
    brow = sb("brow", [1, 2, 6, 128], BF16)
    ones_row = sb("ones_row", [1, 128], BF16)
    convwT = sb("convwT", [128, 2, 6, 31], F32)
    vecT = sb("vecT", [128, 2, 6, 4], F32)
    gT = sb("gT", [128, 8, 16], F32)
    vbc = sb("vbc", [128, 768], F32)
    xa_prev = sb("xa_prev", [128, 2, 512], BF16)
    cext = sb("cext", [128, 2, 6, 30], F32)
    sp_b = sb("sp_b", [128, 2, 512], BF16)
    stat = sb("stat", [128, 64], F32)
    epsc = sb("epsc", [128, 2], F32)
    ps = nc.alloc_psum_tensor("ps", [128, 8, 512], F32)

    hT = blkB[:, 0:4096].bitcast(BF16).rearrange("p (k n) -> p k n", k=16)
    mixT = blkB[:, 4096:8192].bitcast(BF16).rearrange("p (k n) -> p k n", k=16)
    f_fm = blkB[:, 0:8192].rearrange("p (k n) -> p k n", k=16)
    hid = blkA[:, 0:11264].bitcast(BF16).rearrange("p (k n) -> p k n", k=44)
    m_fm = blkA[:, 0:8192].rearrange("p (k n) -> p k n", k=16)
    o = 0
    ext = blkA[:, o:o + 3252].rearrange("p (c n) -> p c n", c=6); o += 3252
    ext_s = blkA[:, o:o + 1488].rearrange("p (c b n) -> p c b n", c=6, b=4); o += 1488
    ycv = blkA[:, o:o + 3072].rearrange("p (c n) -> p c n", c=6); o += 3072
    xa_b = blkA[:, o:o + 1024].bitcast(BF16).rearrange("p (i n) -> p i n", i=4); o += 1024
    v_b = blkA[:, o:o + 1536].bitcast(BF16).rearrange("p (i n) -> p i n", i=4); o += 1536
    xa_f = blkA[:, o:o + 512]; o += 512
    v_f = blkA[:, o:o + 768]; o += 768
    hs_tok = blkA[:, o:o + 768]; o += 768
    u_sb = blkA[:, o:o + 512]; o += 512
    sg = blkA[:, o:o + 512]; o += 512
    yn_b = blkA[:, o:o + 384].bitcast(BF16); o += 384
    pg_b = blkA[:, o:o + 256].bitcast(BF16); o += 256
    sc_nat = blkA[:, o:o + 768]; o += 768
    xa_f2 = blkA[:, o:o + 512]; o += 512
    assert o <= A_WORDS, o
    hs_fm = gbc[:, 0:768].rearrange("p (c n) -> p c n", c=6)
    pm_f = blkA[:, 0:2560].rearrange("p (a n) -> p a n", a=20)
    ws_n = blkA[:, 2560:2560 + 768].rearrange("p (h n) -> p h n", h=6)
    ws_s = blkA[:, 3328:3328 + 768].rearrange("p (h n) -> p h n", h=6)
    pw_f = blkA[:, 4096:4096 + 512].rearrange("p (g n) -> p g n", g=4)
    cw_n = blkA[:, 4608:4608 + 768]
    vc_n = blkA[:, 5376:5376 + 768]
    g_n = blkA[:, 6144:6144 + 128]
    sb_f = blkA[:, 6272:6272 + 1536].rearrange("p (l h n) -> p l h n", l=2, h=6)
    sp_f = blkA[:, 7808:7808 + 1024].rearrange("p (l n) -> p l n", l=2)

    stg = ps[:, 4:8, :].rearrange("p a b -> p (a b)")

    def psb(b):
        return ps[:, b, :].bitcast(BF16)

    stack = ExitStack()
    S = Sched(nc, stack)
    accstate = {"i": 0}

    def acc():
        b = accstate["i"]
        accstate["i"] = (b + 1) % 4
        return b

    def acc2():
        b = 0 if accstate["i"] in (0, 3) else 2
        accstate["i"] = (b + 2) % 4
        return b

    PSK = lambda b: ("ps", b)
    STGK = [("ps", 4), ("ps", 5), ("ps", 6), ("ps", 7)]
    stat_col = {"i": 0}

    def scol(n=1):
        c = stat_col["i"]
        if c + n > 64:
            c = 0
        stat_col["i"] = c + n
        return c

    def dma1(out_ap, in_ap, **kw):
        return lambda e: [e.dma_start(out=out_ap, in_=in_ap, **kw)]

    S.op("sp", dma1(idf[:], ident_d[:, :]), writes=["idf"], dma=1)
    S.op("sp", dma1(pm_f, pmat_d[:, :, :]), writes=["pm_f"], dma=1)
    S.op("sp", dma1(flag[:], flag_d[:, :]), writes=["flag"], dma=1)
    S.op("dve", lambda e: e.tensor_copy(out=idb[:], in_=idf[:]), reads=["idf"], writes=["idb"])
    S.op("dve", lambda e: e.tensor_copy(out=pm_b[:], in_=pm_f), reads=["pm_f"], writes=["pm_b"])
    S.op("dve", lambda e: e.memset(ones_row[:], 1.0), writes=["ones_row"])
    S.op("dve", lambda e: e.memset(epsc[:, 0:1], RMS_EPS), writes=["epsc0"])
    S.op("dve", lambda e: e.memset(epsc[:, 1:2], LN_EPS), writes=["epsc1"])
    S.op("dve", lambda e: e.memset(cext[:], 0.0), writes=[("cext", 0), ("cext", 1)])
    S.op("dve", lambda e: e.memset(xa_prev[:], 0.0), writes=[("xa_prev", 0), ("xa_prev", 1)])
    S.op("sp", dma1(g_n, g4_d.rearrange("n (kc p) -> (n kc) p", p=128)), writes=["g_n"], dma=1)
    b = acc()
    S.op("pe", lambda e, b=b: e.transpose(ps[:, b, 0:128], g_n, idf[:]), reads=["g_n", "idf"], writes=[PSK(b)])
    S.op("dve", lambda e, b=b: e.tensor_copy(out=gT[:].rearrange("p a b -> p (a b)"), in_=ps[:, b, 0:128]),
         reads=[PSK(b)], writes=["gT"])
    S.op("sp", dma1(sb_f[0:1], sgu_b_d.rearrange("(o l) h n -> o l h n", o=1)), writes=["sb_f"], dma=1)
    S.op("sp", dma1(sp_f[0:60], spool_d.rearrange("l r n -> r l n")), writes=["sp_f"], dma=1)
    S.op("dve", lambda e: e.memset(sp_b[:], 0.0), writes=["sp_b"])
    S.op("dve", lambda e: e.tensor_copy(out=sp_b[0:60], in_=sp_f[0:60]), reads=["sp_f", "sp_b"], writes=["sp_b"])
    for l in range(DEPTH):
        S.op("sp", dma1(pw_f, pool_w_d[l].rearrange("g c d -> c g d")), writes=["pw_f"], dma=1)
        S.op("dve", lambda e, l=l: e.tensor_copy(out=poolw_b[:, l], in_=pw_f), reads=["pw_f"], writes=[("poolw", l)])
        S.op("sp", dma1(ws_n, sgu_ws_d[l].rearrange("h i j -> i h j")), writes=["ws_n"], dma=1)
        S.op("dve", lambda e: e.memset(ws_n[0:64, :, 64:128], 0.0), reads=[], writes=["ws_n"])
        S.op("dve", lambda e: e.memset(ws_s, 0.0), writes=["ws_s"])

        def ld_ws_s(e, l=l):
            r = []
            for bb in range(4):
                r.append(e.dma_start(out=ws_s[32 * bb:32 * bb + 32, :, 32 * bb:32 * bb + 32],
                                     in_=sgu_ws_d[l, :, 0:32, 0:32].rearrange("h i j -> i h j")))
            return r
        S.op("sp", ld_ws_s, writes=["ws_s"], dma=4)
        for kind, src, key in ((0, ws_n, "ws_n"), (1, ws_s, "ws_s")):
            b0 = acc2()

            def tr6(e, src=src, b0=b0):
                for h in range(6):
                    r = e.transpose(ps[:, b0 + h // 4, (h % 4) * 128:(h % 4 + 1) * 128], src[:, h, :], idf[:])
                return r
            S.op("pe", tr6, reads=[key, "idf"], writes=[PSK(b0), PSK(b0 + 1)])
            S.op("dve", lambda e, l=l, kind=kind, b0=b0: e.tensor_copy(
                out=wmT[:, l, kind].rearrange("p h n -> p (h n)"),
                in_=ps[:, b0:b0 + 2, :].rearrange("p a b -> p (a b)")[:, 0:768]),
                reads=[PSK(b0), PSK(b0 + 1)], writes=[("wmT", l, kind)])
        S.op("dve", lambda e, l=l: e.tensor_copy(out=brow[0:1, l], in_=sb_f[0:1, l]), reads=["sb_f"], writes=[("brow", l)])
        S.op("sp", dma1(cw_n[0:31], conv_w_d[l]), writes=["cw_n"], dma=1)
        b = acc()

        def trcw(e, b=b):
            for c in range(6):
                r = e.transpose(ps[:, b, c * 32:c * 32 + 31], cw_n[0:31, c * 128:(c + 1) * 128], idf[0:31, 0:31])
            return r
        S.op("pe", trcw, reads=["cw_n", "idf"], writes=[PSK(b)])
        S.op("dve", lambda e, l=l, b=b: e.tensor_copy(out=convwT[:, l], in_=ps[:, b, 0:192].rearrange("p (c k) -> p c k", c=6)[:, :, 0:31]),
             reads=[PSK(b)], writes=[("convwT", l)])
        S.op("sp", dma1(vc_n[0:4], vecs_d[l]), writes=["vc_n"], dma=1)
        b = acc()

        def trvc(e, b=b):
            for c in range(6):
                r = e.transpose(ps[:, b, c * 4:c * 4 + 4], vc_n[0:4, c * 128:(c + 1) * 128], idf[0:4, 0:4])
            return r
        S.op("pe", trvc, reads=["vc_n", "idf"], writes=[PSK(b)])
        S.op("dve", lambda e, l=l, b=b: e.tensor_copy(out=vecT[:, l].rearrange("p c k -> p (c k)"), in_=ps[:, b, 0:24]),
             reads=[PSK(b)], writes=[("vecT", l)])

    SETUP_KEYS = ["pm_f", "ws_n", "ws_s", "pw_f", "cw_n", "vc_n", "g_n", "sb_f", "sp_f"]

    wstate = {"i": 0}

    def wload(parts):
        s = wstate["i"]
        wstate["i"] = (s + 1) % NSLOT
        slot = wring[s]

        def fn(e, slot=slot, parts=parts):
            return [e.dma_start(out=dv(slot), in_=src) for dv, src in parts]
        S.op("pool", fn, writes=[("w", s)], dma=len(parts), wslot=s)
        return slot, ("w", s)

    def wtile_cols(wd_l, c0, w):
        src = wd_l.rearrange("(kc p) n -> p kc n", p=128)[:, :, c0:c0 + w]
        return [(lambda slot, w=w: slot[:, 0:16 * w].rearrange("p (kc n) -> p kc n", kc=16), src)]

    def slot_view(slot, w):
        return slot[:, 0:16 * w].rearrange("p (kc n) -> p kc n", kc=16)

    def rstd_chain(ssq_key, ssq_ap, n, eps_col, out_ap, out_key):
        S.op("act", lambda e: e.activation(out=out_ap, in_=ssq_ap, func=AF.Sqrt, bias=epsc[:, eps_col:eps_col + 1], scale=1.0 / n),
             reads=[ssq_key, "epsc0", "epsc1"], writes=[out_key], selfsync=True)
        S.op("dve", lambda e: e.reciprocal(out=out_ap, in_=out_ap), reads=[out_key], writes=[out_key])

    def prenorm_to_hT(tiles, gidx):
        for i, t in enumerate(tiles):
            c = scol(2)
            kq, kr = ("st", c), ("st", c + 1)
            S.op("act", lambda e, i=i, c=c: e.activation(out=junk[:], in_=xbuf[:, i, :], func=AF.Square, accum_out=stat[:, c:c + 1]),
                 reads=[("x", i)], writes=["junk", kq])
            rstd_chain(kq, stat[:, c:c + 1], D, 0, stat[:, c + 1:c + 2], kr)
            S.op("act", lambda e, i=i, c=c: e.activation(out=hb[:], in_=xbuf[:, i, :], func=AF.Identity, scale=stat[:, c + 1:c + 2]),
                 reads=[("x", i), kr], writes=["hb"])
            for half in range(2):
                b = acc()

                def tr8(e, half=half, b=b):
                    for q in range(8):
                        kc = half * 8 + q
                        r = e.transpose(psb(b)[:, q * 128:(q + 1) * 128], hb[:, kc * 128:(kc + 1) * 128], idb[:])
                    return r
                S.op("pe", tr8, reads=["hb", "idb"], writes=[PSK(b)])
                gsl = gT[:, gidx, half * 8:half * 8 + 8]
                S.op("dve", lambda e, i=i, half=half, b=b, gsl=gsl: e.tensor_tensor(
                    out=hT[:, half * 8:half * 8 + 8, i * 128:(i + 1) * 128],
                    in0=psb(b).rearrange("p (k n) -> p k n", k=8),
                    in1=gsl.unsqueeze(2).broadcast_to([128, 8, 128]), op=ALU.mult),
                    reads=[PSK(b), "gT"], writes=[("hT", i, half)])

    def postnorm_residual(tiles, src_fm, src_keyname, gidx, l):
        S.alias([("hs_fm", c) for c in range(6)], ["gbc"])
        S.op("sp", dma1(gbc[:], bass.AP(g4_d.tensor, gidx * D, [[0, 128], [1, D]])), writes=["gbc"], dma=1)
        for i, t in enumerate(tiles):
            def tr16(e, i=i):
                for j in range(16):
                    r = e.transpose(stg[:, j * 128:(j + 1) * 128], src_fm[:, j, i * 128:(i + 1) * 128], idf[:])
                return r
            S.op("pe", tr16, reads=[(src_keyname, j, i) for j in range(16)] + ["idf"], writes=STGK)
            c = scol(2)
            kq, kr = ("st", c), ("st", c + 1)
            S.op("act", lambda e, c=c: e.activation(out=junk[:], in_=stg, func=AF.Square, accum_out=stat[:, c:c + 1]),
                 reads=STGK, writes=["junk", kq])
            rstd_chain(kq, stat[:, c:c + 1], D, 0, stat[:, c + 1:c + 2], kr)
            tmpv = src_fm[:, :, i * 128:(i + 1) * 128]
            S.op("dve", lambda e, c=c, tmpv=tmpv: e.scalar_tensor_tensor(
                out=tmpv, in0=stg.rearrange("p (k n) -> p k n", k=16), scalar=stat[:, c + 1:c + 2],
                in1=gbc[:].rearrange("p (k n) -> p k n", k=16), op0=ALU.mult, op1=ALU.mult),
                reads=STGK + [kr, "gbc"], writes=[(src_keyname, j, i) for j in range(16)])
            S.op("dve", lambda e, i=i, tmpv=tmpv: e.tensor_tensor(
                out=xbuf[:, i, :].rearrange("p (k n) -> p k n", k=16), in0=xbuf[:, i, :].rearrange("p (k n) -> p k n", k=16),
                in1=tmpv, op=ALU.add),
                reads=[(src_keyname, j, i) for j in range(16)] + [("x", i)], writes=[("x", i)])

    MIXTMP_KEYS = ([("ext", c) for c in range(6)] + [("exts", c) for c in range(6)] + [("ycv", c) for c in range(6)] + [("ycvs", c) for c in range(6)]
                   + [("xa_b", i) for i in range(4)] + [("v_b", i) for i in range(4)]
                   + ["xa_f", "xa_f2", "v_f", "hs_tok", "u_sb", "sg", "yn_b", "pg_b", "sc_nat"])
    MFM_KEYS = [("mfm", j, i) for j in range(16) for i in range(4)]
    HID_KEYS = [("hid", j) for j in range(NFC)]
    FFM_KEYS = [("ffm", j, i) for j in range(16) for i in range(4)]
    BACT_KEYS = [("hT", i, h) for i in range(4) for h in range(2)] + [("mixT", c, i) for c in range(16) for i in range(4)]

    def layer(gi, tiles, l):
        nt = len(tiles)
        N = 128 * nt
        has_s = SAMPLE_T in tiles
        npt = nt - (1 if has_s else 0)
        Np = 128 * npt
        first = (gi == 0)
        w_in_l, w_out_l, w_gate_l, w_up_l, w_down_l = w_in_d[l], w_out_d[l], w_gate_d[l], w_up_d[l], w_down_d[l]
        hT_all = [("hT", i, h) for i in range(nt) for h in range(2)]

        S.new_phase()
        S.alias(FFM_KEYS, BACT_KEYS)
        S.alias(HID_KEYS + MFM_KEYS + SETUP_KEYS, MIXTMP_KEYS)
        S.alias(["gbc"], [("hs_fm", c) for c in range(6)])

        stage(1)
        S.op("sp", dma1(vbc[:], bass.AP(sgu_norm_d.tensor, l * 768, [[0, 128], [1, 768]])), writes=["vbc"], dma=1)
        prenorm_to_hT(tiles, l * 4 + 0)

        pending = []

        def pend(fn, reads=(), writes=()):
            pending.append((fn, reads, writes))

        def drain(n):
            for _ in range(min(n, len(pending))):
                fn, r, w = pending.pop(0)
                S.op("dve", fn, reads=r, writes=w)

        def Sdve(fn, reads=(), writes=(), **kw):
            ev = S.op("dve", fn, reads=reads, writes=writes, **kw)
            drain(10)
            return ev

        stage(5)
        S.op("dve", lambda e: e.tensor_copy(out=ext[:, :, 0:30], in_=cext[:, l]), reads=[("cext", l)], writes=[("ext", c) for c in range(6)])
        if has_s:
            S.op("sp", dma1(sc_nat[0:120], sconv_d[l]), writes=["sc_nat"], dma=1)
            for c in range(6):
                b = acc()
                S.op("pe", lambda e, c=c, b=b: e.transpose(ps[:, b, 0:120], sc_nat[0:120, c * 128:(c + 1) * 128], idf[0:120, 0:120]),
                     reads=["sc_nat", "idf"], writes=[PSK(b)])
                S.op("dve", lambda e, c=c, b=b: e.tensor_copy(out=ext_s[:, c, :, 0:30], in_=ps[:, b, 0:120].rearrange("p (b r) -> p b r", b=4)),
                     reads=[PSK(b)], writes=[("exts", c)])
        for c in range(6):
            slot, wk = wload([
                (lambda s: slot_view(s, 256)[:, :, 0:128], w_in_l.rearrange("(kc p) n -> p kc n", p=128)[:, :, 2048 + 128 * c:2048 + 128 * c + 128]),
                (lambda s: slot_view(s, 256)[:, :, 128:256], w_in_l.rearrange("(kc p) n -> p kc n", p=128)[:, :, 2816 + 128 * c:2816 + 128 * c + 128]),
            ])
            bA, bG = acc(), acc()

            def mm_cc(e, slot=slot, bA=bA, bG=bG):
                sv = slot_view(slot, 256)
                for bb, c0 in ((bA, 0), (bG, 128)):
                    for kc in range(16):
                        r = e.matmul(ps[:, bb, 0:N], sv[:, kc, c0:c0 + 128], hT[:, kc, 0:N], start=(kc == 0), stop=(kc == 15))
                return r
            S.op("pe", mm_cc, reads=hT_all + [wk], writes=[PSK(bA), PSK(bG)])
            S.op("act", lambda e, bG=bG: e.activation(out=sg[:, 0:N], in_=ps[:, bG, 0:N], func=AF.Sigmoid), reads=[PSK(bG)], writes=["sg"])
            S.op("dve", lambda e, c=c, bA=bA: e.tensor_tensor(out=ext[:, c, 30:30 + Np], in0=ps[:, bA, 0:Np], in1=sg[:, 0:Np], op=ALU.mult),
                 reads=[PSK(bA), "sg"], writes=[("ext", c)])
            if has_s:
                S.op("dve", lambda e, c=c, bA=bA: e.tensor_tensor(out=hs_fm[:, c, :], in0=ps[:, bA, Np:N], in1=sg[:, Np:N], op=ALU.mult),
                     reads=[PSK(bA), "sg"], writes=[("hs_fm", c)])
                S.op("dve", lambda e, c=c: e.tensor_copy(out=ext_s[:, c, :, 30:62], in_=hs_fm[:, c, :].rearrange("p (b j) -> p b j", b=4)),
                     reads=[("hs_fm", c)], writes=[("exts", c)])
            if first:
                S.op("dve", lambda e, c=c: e.tensor_scalar(out=ext[:, c, 30:158], in0=ext[:, c, 30:158], scalar1=flag[:, 0:1], scalar2=None, op0=ALU.mult),
                     reads=[("ext", c), "flag"], writes=[("ext", c)])

            ys_ = ycv[:, c, Np:N].rearrange("p (b j) -> p b j", b=4) if has_s else None
            for k in range(31):
                pass
            for k in range(31):
                if k == 0:
                    pend(lambda e, c=c: e.tensor_scalar(out=ycv[:, c, 0:Np], in0=ext[:, c, 0:Np], scalar1=convwT[:, l, c, 0:1],
                                                               scalar2=vecT[:, l, c, 1:2], op0=ALU.mult, op1=ALU.add),
                         reads=[("ext", c), ("convwT", l), ("vecT", l)], writes=[("ycv", c)])
                    if has_s:
                        pend(lambda e, c=c, ys_=ys_: e.tensor_scalar(out=ys_, in0=ext_s[:, c, :, 0:32], scalar1=convwT[:, l, c, 0:1],
                                                                            scalar2=vecT[:, l, c, 1:2], op0=ALU.mult, op1=ALU.add),
                             reads=[("exts", c), ("convwT", l), ("vecT", l)], writes=[("ycvs", c)])
                else:
                    pend(lambda e, c=c, k=k: e.scalar_tensor_tensor(out=ycv[:, c, 0:Np], in0=ext[:, c, k:k + Np], scalar=convwT[:, l, c, k:k + 1],
                                                                           in1=ycv[:, c, 0:Np], op0=ALU.mult, op1=ALU.add),
                         reads=[("ext", c), ("ycv", c)], writes=[("ycv", c)])
                    if has_s:
                        pend(lambda e, c=c, k=k, ys_=ys_: e.scalar_tensor_tensor(out=ys_, in0=ext_s[:, c, :, k:k + 32], scalar=convwT[:, l, c, k:k + 1],
                                                                                        in1=ys_, op0=ALU.mult, op1=ALU.add),
                             reads=[("exts", c), ("ycvs", c)], writes=[("ycvs", c)])
        S.op("dve", lambda e: e.tensor_copy(out=cext[:, l], in_=ext[:, :, Np:Np + 30]), reads=[("ext", c) for c in range(6)], writes=[("cext", l)])
        if LASTP_T in tiles:
            ip = tiles.index(LASTP_T)

            def tr_hp(e, ip=ip):
                for c in range(6):
                    r = e.transpose(stg[:, c * 128:(c + 1) * 128], ext[:, c, 30 + ip * 128:30 + (ip + 1) * 128], idf[:])
                return r
            S.op("pe", tr_hp, reads=[("ext", c) for c in range(6)] + ["idf"], writes=STGK[0:2])
            S.op("dve", lambda e: e.tensor_copy(out=hs_tok, in_=stg[:, 0:768]), reads=STGK[0:2], writes=["hs_tok"])
            S.op("sp", dma1(nconv_p_d[l], hs_tok), reads=["hs_tok"], dma=1)
        if has_s:
            def tr_hs(e):
                for c in range(6):
                    r = e.transpose(stg[:, c * 128:(c + 1) * 128], hs_fm[:, c, :], idf[:])
                return r
            S.op("pe", tr_hs, reads=[("hs_fm", c) for c in range(6)] + ["idf"], writes=STGK[0:2])
            S.op("dve", lambda e: e.tensor_copy(out=hs_tok, in_=stg[:, 0:768]), reads=STGK[0:2], writes=["hs_tok"])

            S.op("sp", dma1(nconv_s_d[l], hs_tok), reads=["hs_tok"], dma=1)

        stage(2)
        sl0, k0 = wload(wtile_cols(w_in_l, 0, 256))
        sl1, k1 = wload(wtile_cols(w_in_l, 256, 256))
        for i, t in enumerate(tiles):
            b = acc()

            def mm_xa(e, i=i, b=b):
                for half, sl in ((0, sl0), (1, sl1)):
                    sv = slot_view(sl, 256)
                    for kc in range(16):
                        r = e.matmul(ps[:, b, half * 256:(half + 1) * 256], hT[:, kc, i * 128:(i + 1) * 128], sv[:, kc, :],
                                     start=(kc == 0), stop=(kc == 15))
                return r
            S.op("pe", mm_xa, reads=[("hT", i, 0), ("hT", i, 1), k0, k1], writes=[PSK(b)])
            if t == HALO_T:
                Sdve(lambda e, i=i, b=b: e.tensor_scalar(out=xa_b[:, i, :], in0=ps[:, b, :], scalar1=flag[:, 0:1], scalar2=None, op0=ALU.mult),
                     reads=[PSK(b), "flag"], writes=[("xa_b", i)])
            else:
                Sdve(lambda e, i=i, b=b: e.tensor_copy(out=xa_b[:, i, :], in_=ps[:, b, :]), reads=[PSK(b)], writes=[("xa_b", i)])
            if t in (LASTP_T, SAMPLE_T) and not ('noxaf16' in _DBG and t == LASTP_T) and not ('noxaf17' in _DBG and t == SAMPLE_T):
                xaf, xak = (xa_f, "xa_f") if t == LASTP_T else (xa_f2, "xa_f2")
                Sdve(lambda e, b=b, xaf=xaf: e.tensor_copy(out=xaf, in_=ps[:, b, :]), reads=[PSK(b)], writes=[xak])
                if 'nopoolst' not in _DBG:
                    S.op("sp", dma1(npool_p_d[l] if t == LASTP_T else npool_s_d[l], xaf), reads=[xak], dma=1)
            if t == SAMPLE_T:
                cur, prv, K2 = 12, 16, 128
                prev_ap = lambda g: sp_b[:, l, g * 128:(g + 1) * 128]
                prev_key = "sp_b"
                if 'genmat' in _DBG:
                    cur, prv = 0, 4
                if 'noprevs' in _DBG:
                    prev_ap = lambda g: xa_prev[:, l, g * 128:(g + 1) * 128]
                    prev_key = ("xa_prev", l)
            else:
                cur = 8 if t == 1 else 0
                prv, K2 = 4, 128
                if i == 0:
                    prev_ap = lambda g: xa_prev[:, l, g * 128:(g + 1) * 128]
                    prev_key = ("xa_prev", l)
                else:
                    prev_ap = lambda g, i=i: xa_b[:, i - 1, g * 128:(g + 1) * 128]
                    prev_key = ("xa_b", i - 1)
            b2 = acc()

            def mm_pool1(e, i=i, b2=b2, cur=cur, prv=prv, K2=K2, prev_ap=prev_ap):
                for g in range(4):
                    e.matmul(ps[:, b2, g * 128:(g + 1) * 128], xa_b[:, i, g * 128:(g + 1) * 128], pm_b[:, cur + g, :], start=True, stop=False)
                    r = e.matmul(ps[:, b2, g * 128:(g + 1) * 128], prev_ap(g), pm_b[0:K2, prv + g, :], start=False, stop=True)
                return r
            S.op("pe", mm_pool1, reads=[("xa_b", i), prev_key, "pm_b"], writes=[PSK(b2)])
            Sdve(lambda e, b2=b2: e.tensor_copy(out=pg_b, in_=ps[:, b2, :]), reads=[PSK(b2)], writes=["pg_b"])
            b3 = acc()

            def mm_pool2(e, b3=b3):
                for g in range(4):
                    r = e.matmul(ps[:, b3, g * 128:(g + 1) * 128], poolw_b[:, l, g, :], pg_b[:, g * 128:(g + 1) * 128], start=True, stop=True)
                return r
            S.op("pe", mm_pool2, reads=["pg_b", ("poolw", l)], writes=[PSK(b3)])

            def ev_pool(e, i=i, b3=b3):
                for g in range(4):
                    r = e.activation(out=mixT[:, g, i * 128:(i + 1) * 128], in_=ps[:, b3, g * 128:(g + 1) * 128], func=AF.Identity,
                                     scale=vecT[:, l, g, 0:1])
                return r
            S.op("act", ev_pool, reads=[PSK(b3), ("vecT", l)], writes=[("mixT", g, i) for g in range(4)])
        Sdve(lambda e: e.tensor_copy(out=xa_prev[:, l, :], in_=xa_b[:, npt - 1, :]), reads=[("xa_b", npt - 1)], writes=[("xa_prev", l)])

        stage(3)
        slv = [wload(wtile_cols(w_in_l, 1280 + 256 * q, 256)) for q in range(3)]
        for i, t in enumerate(tiles):
            b0 = acc2()
            pv = ps[:, b0:b0 + 2, :].rearrange("p a b -> p (a b)")

            def mm_v(e, i=i, pv=pv):
                for q in range(3):
                    sv = slot_view(slv[q][0], 256)
                    for kc in range(16):
                        r = e.matmul(pv[:, q * 256:(q + 1) * 256], hT[:, kc, i * 128:(i + 1) * 128], sv[:, kc, :], start=(kc == 0), stop=(kc == 15))
                return r
            S.op("pe", mm_v, reads=[("hT", i, 0), ("hT", i, 1)] + [k for _, k in slv], writes=[PSK(b0), PSK(b0 + 1)])
            c = scol(2)
            kq, kr = ("st", c), ("st", c + 1)
            S.op("act", lambda e, pv=pv, c=c: e.activation(out=junk[:, 0:768], in_=pv[:, 0:768], func=AF.Square, accum_out=stat[:, c:c + 1]),
                 reads=[PSK(b0), PSK(b0 + 1)], writes=["junk", kq])
            rstd_chain(kq, stat[:, c:c + 1], 768, 0, stat[:, c + 1:c + 2], kr)
            if t == SAMPLE_T:
                Sdve(lambda e, pv=pv, c=c: e.scalar_tensor_tensor(out=v_f, in0=pv[:, 0:768], scalar=stat[:, c + 1:c + 2], in1=vbc[:],
                                                                         op0=ALU.mult, op1=ALU.mult),
                     reads=[PSK(b0), PSK(b0 + 1), kr, "vbc"], writes=["v_f"])
                Sdve(lambda e, i=i: e.tensor_copy(out=v_b[:, i, :], in_=v_f), reads=["v_f"], writes=[("v_b", i)])
                S.op("sp", dma1(nv_s_d[l], v_f), reads=["v_f"], dma=1)
            else:
                Sdve(lambda e, pv=pv, c=c, i=i: e.scalar_tensor_tensor(out=v_b[:, i, :], in0=pv[:, 0:768], scalar=stat[:, c + 1:c + 2],
                                                                              in1=vbc[:], op0=ALU.mult, op1=ALU.mult),
                     reads=[PSK(b0), PSK(b0 + 1), kr, "vbc"], writes=[("v_b", i)])

        stage(4)
        slu = [wload(wtile_cols(w_in_l, 512 + 256 * q, 256)) for q in range(3)]
        for h in range(6):
            slot, wk = slu[h // 2]
            c0 = (h % 2) * 128
            bU = acc()

            def mm_u(e, slot=slot, c0=c0, bU=bU):
                sv = slot_view(slot, 256)
                for kc in range(16):
                    r = e.matmul(ps[:, bU, 0:N], sv[:, kc, c0:c0 + 128], hT[:, kc, 0:N], start=(kc == 0), stop=(kc == 15))
                return r
            S.op("pe", mm_u, reads=hT_all + [wk], writes=[PSK(bU)])
            bM = acc()

            def mm_sgu(e, h=h, bM=bM):
                for i, t in enumerate(tiles):
                    if t == SAMPLE_T:
                        e.matmul(ps[:, bM, i * 128:(i + 1) * 128], v_b[:, i, h * 128:(h + 1) * 128], wmT[:, l, 1, h, :], start=True, stop=False)
                        for bb in range(4):
                            r = e.matmul(ps[:, bM, i * 128 + 32 * bb:i * 128 + 32 * bb + 32], ones_row[0:1, :], brow[0:1, l, h, 0:32],
                                         start=False, stop=(bb == 3))
                    else:
                        e.matmul(ps[:, bM, i * 128:(i + 1) * 128], v_b[:, i, h * 128:(h + 1) * 128], wmT[:, l, 0, h, :], start=True, stop=False)
                        r = e.matmul(ps[:, bM, i * 128:(i + 1) * 128], ones_row[0:1, :], brow[0:1, l, h, :], start=False, stop=True)
                return r
            S.op("pe", mm_sgu, reads=[("v_b", i) for i in range(nt)] + [("wmT", l, 0), ("wmT", l, 1), ("brow", l), "ones_row"], writes=[PSK(bM)])
            S.op("act", lambda e, bU=bU: e.activation(out=u_sb[:, 0:N], in_=ps[:, bU, 0:N], func=AF.Copy), reads=[PSK(bU)], writes=["u_sb"])
            Sdve(lambda e, h=h, bM=bM: e.tensor_tensor(out=mixT[:, 4 + h, 0:N], in0=u_sb[:, 0:N], in1=ps[:, bM, 0:N], op=ALU.mult),
                 reads=["u_sb", PSK(bM)], writes=[("mixT", 4 + h, i) for i in range(nt)])

        drain(len(pending))

        stage(6)
        for i, t in enumerate(tiles):
            def tr_y(e, i=i):
                for c in range(6):
                    r = e.transpose(stg[:, c * 128:(c + 1) * 128], ycv[:, c, i * 128:(i + 1) * 128], idf[:])
                return r
            S.op("pe", tr_y, reads=[("ycv", c) for c in range(6)] + [("ycvs", c) for c in range(6)] + ["idf"], writes=STGK[0:2])
            c0 = scol(6)
            k1_, k2_ = ("st", c0), ("st", c0 + 1)
            S.op("act", lambda e, c0=c0: e.activation(out=junk[:, 0:768], in_=stg[:, 0:768], func=AF.Identity, accum_out=stat[:, c0:c0 + 1]),
                 reads=STGK[0:2], writes=["junk", k1_])
            S.op("act", lambda e, c0=c0: e.activation(out=junk[:, 0:768], in_=stg[:, 0:768], func=AF.Square, accum_out=stat[:, c0 + 1:c0 + 2]),
                 reads=STGK[0:2], writes=["junk", k2_])
            km, kv, kr, kn = ("st", c0 + 2), ("st", c0 + 3), ("st", c0 + 4), ("st", c0 + 5)
            S.op("dve", lambda e, c0=c0: e.tensor_scalar(out=stat[:, c0 + 2:c0 + 3], in0=stat[:, c0:c0 + 1], scalar1=1.0 / 768, scalar2=None, op0=ALU.mult),
                 reads=[k1_], writes=[km])
            S.op("dve", lambda e, c0=c0: e.scalar_tensor_tensor(out=stat[:, c0 + 3:c0 + 4], in0=stat[:, c0 + 2:c0 + 3], scalar=-1.0, in1=stat[:, c0 + 2:c0 + 3],
                                                                op0=ALU.mult, op1=ALU.mult),
                 reads=[km], writes=[kv], selfsync=True)
            S.op("dve", lambda e, c0=c0: e.scalar_tensor_tensor(out=stat[:, c0 + 3:c0 + 4], in0=stat[:, c0 + 1:c0 + 2], scalar=1.0 / 768, in1=stat[:, c0 + 3:c0 + 4],
                                                                op0=ALU.mult, op1=ALU.add),
                 reads=[k2_, kv], writes=[kv], selfsync=True)
            S.op("act", lambda e, c0=c0: e.activation(out=stat[:, c0 + 4:c0 + 5], in_=stat[:, c0 + 3:c0 + 4], func=AF.Sqrt, bias=epsc[:, 1:2], scale=1.0),
                 reads=[kv, "epsc1"], writes=[kr])
            S.op("dve", lambda e, c0=c0: e.reciprocal(out=stat[:, c0 + 4:c0 + 5], in_=stat[:, c0 + 4:c0 + 5]), reads=[kr], writes=[kr])
            S.op("dve", lambda e, c0=c0: e.scalar_tensor_tensor(out=stat[:, c0 + 5:c0 + 6], in0=stat[:, c0 + 2:c0 + 3], scalar=-1.0, in1=stat[:, c0 + 4:c0 + 5],
                                                                op0=ALU.mult, op1=ALU.mult),
                 reads=[km, kr], writes=[kn], selfsync=True)
            S.op("dve", lambda e, c0=c0: e.tensor_scalar(out=yn_b, in0=stg[:, 0:768], scalar1=stat[:, c0 + 4:c0 + 5], scalar2=stat[:, c0 + 5:c0 + 6],
                                                         op0=ALU.mult, op1=ALU.add),
                 reads=STGK[0:2] + [kr, kn], writes=["yn_b"], selfsync=True)
            b = acc()

            def tr_yn(e, b=b):
                for c in range(6):
                    r = e.transpose(psb(b)[:, c * 128:(c + 1) * 128], yn_b[:, c * 128:(c + 1) * 128], idb[:])
                return r
            S.op("pe", tr_yn, reads=["yn_b", "idb"], writes=[PSK(b)])

            def ev_yc(e, i=i, b=b):
                for c in range(6):
                    r = e.activation(out=mixT[:, 10 + c, i * 128:(i + 1) * 128], in_=psb(b)[:, c * 128:(c + 1) * 128], func=AF.Silu,
                                     bias=vecT[:, l, c, 3:4], scale=vecT[:, l, c, 2:3])
                return r
            S.op("act", ev_yc, reads=[PSK(b), ("vecT", l)], writes=[("mixT", 10 + c, i) for c in range(6)])

        stage(7)
        S.alias(MIXTMP_KEYS + [("hs_fm", c) for c in range(6)], MFM_KEYS)
        mixT_all = [("mixT", c, i) for c in range(16) for i in range(nt)]
        for jb in range(8):
            slot, wk = wload(wtile_cols(w_out_l, 256 * jb, 256))
            for cc in range(2):
                j = 2 * jb + cc
                b = acc()

                def mm_o(e, slot=slot, cc=cc, b=b):
                    sv = slot_view(slot, 256)
                    for kc in range(16):
                        r = e.matmul(ps[:, b, 0:N], sv[:, kc, cc * 128:(cc + 1) * 128], mixT[:, kc, 0:N], start=(kc == 0), stop=(kc == 15))
                    return r
                S.op("pe", mm_o, reads=mixT_all + [wk], writes=[PSK(b)])
                if j % 2 == 0:
                    S.op("act", lambda e, j=j, b=b: e.activation(out=m_fm[:, j, 0:N], in_=ps[:, b, 0:N], func=AF.Copy),
                         reads=[PSK(b)], writes=[("mfm", j, i) for i in range(nt)])
                else:
                    S.op("dve", lambda e, j=j, b=b: e.tensor_copy(out=m_fm[:, j, 0:N], in_=ps[:, b, 0:N]),
                         reads=[PSK(b)], writes=[("mfm", j, i) for i in range(nt)])
        postnorm_residual(tiles, m_fm, "mfm", l * 4 + 1, l)

        stage(8)
        prenorm_to_hT(tiles, l * 4 + 2)

        stage(9)
        S.alias(MFM_KEYS + MIXTMP_KEYS, HID_KEYS)
        sgbuf = [sg, u_sb]
        sgkey = ["sg", "u_sb"]
        for jb in range(22):
            slg, kg = wload(wtile_cols(w_gate_l, 256 * jb, 256))
            slu_, ku = wload(wtile_cols(w_up_l, 256 * jb, 256))
            bGs, bUs = [], []
            for cc in range(2):
                bG = acc()
                bGs.append(bG)

                def mm_g(e, slg=slg, cc=cc, bG=bG):
                    sv = slot_view(slg, 256)
                    for kc in range(16):
                        r = e.matmul(ps[:, bG, 0:N], sv[:, kc, cc * 128:(cc + 1) * 128], hT[:, kc, 0:N], start=(kc == 0), stop=(kc == 15))
                    return r
                S.op("pe", mm_g, reads=hT_all + [kg], writes=[PSK(bG)])
                S.op("act", lambda e, bG=bG, cc=cc: e.activation(out=sgbuf[cc][:, 0:N], in_=ps[:, bG, 0:N], func=AF.Silu),
                     reads=[PSK(bG)], writes=[sgkey[cc]])
            for cc in range(2):
                j = 2 * jb + cc
                bU = acc()

                def mm_u2(e, slu_=slu_, cc=cc, bU=bU):
                    sv = slot_view(slu_, 256)
                    for kc in range(16):
                        r = e.matmul(ps[:, bU, 0:N], sv[:, kc, cc * 128:(cc + 1) * 128], hT[:, kc, 0:N], start=(kc == 0), stop=(kc == 15))
                    return r
                S.op("pe", mm_u2, reads=hT_all + [ku], writes=[PSK(bU)])
                S.op("dve", lambda e, j=j, bU=bU, cc=cc: e.tensor_tensor(out=hid[:, j, 0:N], in0=sgbuf[cc][:, 0:N], in1=ps[:, bU, 0:N], op=ALU.mult),
                     reads=[sgkey[cc], PSK(bU)], writes=[("hid", j)])

        stage(10)
        S.alias(BACT_KEYS, FFM_KEYS)
        hid_all = [("hid", j) for j in range(NFC)]
        wdv = w_down_l.rearrange("(fc p) n -> p fc n", p=128)
        for j in range(16):
            halves = []
            for hh in range(2):
                halves.append(wload([(lambda s: s[:, 0:22 * 128].rearrange("p (fc n) -> p fc n", fc=22),
                                      wdv[:, hh * 22:(hh + 1) * 22, j * 128:(j + 1) * 128])]))
            b = acc()

            for hh in range(2):
                def mm_d(e, hh=hh, halves=halves, b=b):
                    sv = halves[hh][0][:, 0:22 * 128].rearrange("p (fc n) -> p fc n", fc=22)
                    for f in range(22):
                        fc = hh * 22 + f
                        r = e.matmul(ps[:, b, 0:N], sv[:, f, :], hid[:, fc, 0:N], start=(fc == 0), stop=(fc == NFC - 1))
                    return r
                S.op("pe", mm_d, reads=[("hid", hh * 22 + f) for f in range(22)] + [halves[hh][1]], writes=[PSK(b)])
            if j % 2 == 0:
                S.op("act", lambda e, j=j, b=b: e.activation(out=f_fm[:, j, 0:N], in_=ps[:, b, 0:N], func=AF.Copy),
                     reads=[PSK(b)], writes=[("ffm", j, i) for i in range(nt)])
            else:
                S.op("dve", lambda e, j=j, b=b: e.tensor_copy(out=f_fm[:, j, 0:N], in_=ps[:, b, 0:N]),
                     reads=[PSK(b)], writes=[("ffm", j, i) for i in range(nt)])
        postnorm_residual(tiles, f_fm, "ffm", l * 4 + 3, l)

    try:
        stage(0)
        for gi, tiles in enumerate(groups):
            for i, t in enumerate(tiles):
                S.op("sp", dma1(xbuf[:, i, :], xin[t]), writes=[("x", i)], dma=1)
            for l in range(DEPTH):
                layer(gi, tiles, l)
            for i, t in enumerate(tiles):
                if t == HALO_T:
                    continue
                dst = ys_d[:, :] if t == SAMPLE_T else yp_d[t - 1]
                S.op("sp", dma1(dst, xbuf[:, i, :]), reads=[("x", i)], dma=1)
    except _StopBuild:
        pass

    def fin(e):
        for s, v in S.all_dma.items():
            if s.name.startswith("dq"):
                e.wait_ge(s, v)
        return []
    S.ops["sp"].append(({}, fin, None, True))

    with stack:
        with nc.Block() as block:
            @block.sync
            def _(e):
                S.emit("sp", e)

            @block.gpsimd
            def _(e):
                S.emit("pool", e)

            @block.tensor
            def _(e):
                S.emit("pe", e)

            @block.scalar
            def _(e):
                S.emit("act", e)

            @block.vector
            def _(e):
                S.emit("dve", e)
    return nc


def _pool_mats(core):
    W = (2, 4, 8, 16)
    pm = np.zeros((20, 128, 128), np.float32)
    tp = np.arange(128)[:, None]
    t = np.arange(128)[None, :]
    for g, w in enumerate(W):
        cur = ((t - tp >= 0) & (t - tp < w)).astype(np.float32) / w - (t == tp)
        prev = ((t - (tp - 128)) < w).astype(np.float32) / w
        pm[g] = cur
        pm[4 + g] = prev
        if core == 0:
            cnt = np.minimum(t + 1, w).astype(np.float32)
            pm[8 + g] = ((t - tp >= 0) & (t - tp < w)).astype(np.float32) / cnt - (t == tp)
        else:
            pm[8 + g] = cur
        bi, ii = np.divmod(np.arange(128), 32)
        same = (bi[:, None] == bi[None, :])
        dj = ii[None, :] - ii[:, None]
        pm[12 + g] = (same & (dj >= 0) & (dj < w)).astype(np.float32) / w - np.eye(128, dtype=np.float32)
        hist = np.zeros((128, 128), np.float32)
        rb, rr = np.divmod(np.arange(60), 15)
        m = (rb[:, None] == bi[None, :]) & ((ii[None, :] - (rr[:, None] - 15)) < w)
        hist[:60] = m.astype(np.float32) / w
        pm[16 + g] = hist
    return np.ascontiguousarray(pm.transpose(1, 0, 2))


_NC_CACHE = {}


def kernel(x_prompt, x_sample, state_pool, state_conv, norm_pre_mix, norm_post_mix, norm_pre_ffn, norm_post_ffn,
           w_in, w_out, pool_w, pool_scale, sgu_norm, sgu_ws, sgu_b, conv_w, conv_b, conv_ln_g, conv_ln_b,
           w_gate, w_up, w_down):
    f = lambda a: np.ascontiguousarray(np.asarray(a, dtype=np.float32))
    x_prompt, x_sample, state_pool, state_conv = f(x_prompt), f(x_sample), f(state_pool), f(state_conv)
    xp = x_prompt[0]
    g4 = np.stack([f(norm_pre_mix), f(norm_post_mix), f(norm_pre_ffn), f(norm_post_ffn)], axis=1).reshape(8, D)
    vecs = np.zeros((DEPTH, 4, 768), np.float32)
    vecs[:, 0, :512] = f(pool_scale)
    vecs[:, 1] = f(conv_b)
    vecs[:, 2] = f(conv_ln_g)
    vecs[:, 3] = f(conv_ln_b)
    shared = {
        "ident": np.eye(128, dtype=np.float32),
        "w_in": f(w_in), "w_out": f(w_out), "w_gate": f(w_gate), "w_up": f(w_up), "w_down": f(w_down),
        "g4": np.ascontiguousarray(g4), "pool_w": f(pool_w), "vecs": vecs, "sgu_norm": f(sgu_norm),
        "sgu_ws": f(sgu_ws), "sgu_b": f(sgu_b), "conv_w": f(conv_w),
    }
    in_maps = []
    for c in range(NCORES):
        xin = np.zeros((NT, 128, D), np.float32)
        if c > 0:
            xin[0] = xp[c * 2048 - 128:c * 2048]
        xin[1:17] = xp[c * 2048:(c + 1) * 2048].reshape(16, 128, D)
        xin[17] = x_sample[4 * c:4 * c + 4].reshape(128, D)
        m = dict(shared)
        m["xin"] = xin
        m["flag"] = np.full((128, 1), 0.0 if c == 0 else 1.0, np.float32)
        m["spool"] = np.ascontiguousarray(state_pool[:, 4 * c:4 * c + 4].reshape(DEPTH, 60, 512))
        m["sconv"] = np.ascontiguousarray(state_conv[:, 4 * c:4 * c + 4].reshape(DEPTH, 120, 768))
        m["pmat"] = _pool_mats(c)
        in_maps.append(m)
    if "nc" not in _NC_CACHE:
        _NC_CACHE["nc"] = build_program()
    nc = _NC_CACHE["nc"]
    res = run_bass_kernel_spmd(nc, in_maps, core_ids=list(range(NCORES)))
    R = res.results
    y_prompt = np.concatenate([R[c]["yp"].reshape(2048, D) for c in range(NCORES)], axis=0)[None]
    y_sample = np.concatenate([R[c]["ys"].reshape(4, 32, D) for c in range(NCORES)], axis=0)
    new_pool_p = R[NCORES - 1]["npool_p"][:, None, 113:128]
    new_conv_p = R[NCORES - 1]["nconv_p"][:, None, 98:128]
    new_pool_s = np.concatenate([R[c]["npool_s"].reshape(DEPTH, 4, 32, 512)[:, :, 17:32] for c in range(NCORES)], axis=1)
    new_conv_s = np.concatenate([R[c]["nconv_s"].reshape(DEPTH, 4, 32, 768)[:, :, 2:32] for c in range(NCORES)], axis=1)
    new_v_s = np.concatenate([R[c]["nv_s"].reshape(DEPTH, 4, 32, 768) for c in range(NCORES)], axis=1)
    out = (y_prompt, y_sample, new_pool_p, new_conv_p, new_pool_s, new_conv_s, new_v_s)
    return tuple(np.ascontiguousarray(o, dtype=np.float32) for o in out)
```

```python
from contextlib import ExitStack
import os
_DBG = os.environ.get('KDBG', '')
import numpy as np
import concourse.bass as bass
import concourse.mybir as mybir
from concourse.bass_utils import run_bass_kernel_spmd

F32 = mybir.dt.float32
BF16 = mybir.dt.bfloat16
AF = mybir.ActivationFunctionType
ALU = mybir.AluOpType

NCORES = 8
D = 2048
INW = 3584
DFF = 5632
NKC = 16
NFC = 44
DEPTH = 2
NT = 18
HALO_T, LASTP_T, SAMPLE_T = 0, 16, 17
GROUPS = [[0, 1, 2, 3], [4, 5, 6, 7], [8, 9, 10, 11], [12, 13, 14, 15], [16, 17]]
RMS_EPS = 1e-6
LN_EPS = 1e-5
NSLOT = 4
SLOTW = 4096
NDSEM = 40


class Sched:
    def __init__(self, nc, stack):
        self.nc = nc
        self.stack = stack
        self.engs = ["pe", "act", "dve", "pool", "sp"]
        self.ops = {e: [] for e in self.engs}
        self.semc = 0
        self.msem = {}
        for e in ("pe", "act", "dve", "pool"):
            self._new_msem(e)
        self.dsems = [stack.enter_context(nc.semaphore(f"dq{i}")) for i in range(NDSEM)]
        self.dcnt = [0] * NDSEM
        self.drr = 0
        self.wsems = [stack.enter_context(nc.semaphore(f"wq{i}")) for i in range(NSLOT)]
        self.wcnt = [0] * NSLOT
        self.lastw = {}
        self.readers = {}
        self.all_dma = {}

    def _new_msem(self, e):
        s = self.stack.enter_context(self.nc.semaphore(f"m{e}{self.semc}"))
        self.semc += 1
        self.msem[e] = [s, 0]

    def new_phase(self):
        for e in ("pe", "act", "dve"):
            self._new_msem(e)

    @staticmethod
    def _add(d, ev):
        s, v = ev
        if d.get(s, 0) < v:
            d[s] = v

    def op(self, eng, fn, reads=(), writes=(), dma=0, wslot=None, selfsync=False):
        deps = {}
        for k in reads:
            ev = self.lastw.get(k)
            if ev is not None:
                self._add(deps, ev)
        for k in writes:
            ev = self.lastw.get(k)
            if ev is not None:
                self._add(deps, ev)
            for s, v in self.readers.get(k, {}).items():
                self._add(deps, (s, v))
        if dma:
            if wslot is not None:
                sem = self.wsems[wslot]
                self.wcnt[wslot] += 16 * dma
                val = self.wcnt[wslot]
            else:
                i = self.drr
                self.drr = (self.drr + 1) % NDSEM
                sem = self.dsems[i]
                if self.dcnt[i] > 0:
                    self._add(deps, (sem, self.dcnt[i]))
                self.dcnt[i] += 16 * dma
                val = self.dcnt[i]
            ev = (sem, val)
            self.all_dma[sem] = val
        else:
            ms = self.msem[eng]
            own = ms[0]
            if eng == "pe":
                for s in [s for s in deps if s.name.startswith("m" + eng)]:
                    del deps[s]
            ms[1] += 1
            ev = (own, ms[1])
        self.ops[eng].append((deps, fn, ev, bool(dma)))
        for k in writes:
            self.lastw[k] = ev
            self.readers[k] = {}
        for k in reads:
            self._add(self.readers.setdefault(k, {}), ev)
        return ev

    def alias(self, old_keys, new_keys):
        evs = {}
        for k in old_keys:
            ev = self.lastw.get(k)
            if ev is not None:
                self._add(evs, ev)
            for s, v in self.readers.get(k, {}).items():
                self._add(evs, (s, v))
        for k in new_keys:
            r = self.readers.setdefault(k, {})
            for s, v in evs.items():
                self._add(r, (s, v))

    def emit(self, eng_name, eng):
        seen = {}
        for deps, fn, ev, is_dma in self.ops[eng_name]:
            for s, v in deps.items():
                if seen.get(s, 0) < v:
                    eng.wait_ge(s, v)
                    seen[s] = v
            r = fn(eng)
            if is_dma:
                for ins in r:
                    ins.then_inc(ev[0], 16)
            else:
                r.then_inc(ev[0], 1)


class _StopBuild(Exception):
    pass


def build_program(groups=None, stop_after=None):
    groups = GROUPS if groups is None else groups

    def stage(k):
        if stop_after is not None and k > stop_after:
            raise _StopBuild()
    nc = bass.Bass("TRN2", target_bir_lowering=False)

    def din(name, shape):
        return nc.dram_tensor(name, list(shape), F32, kind="ExternalInput").ap()

    def dout(name, shape):
        return nc.dram_tensor(name, list(shape), F32, kind="ExternalOutput").ap()

    xin = din("xin", [NT, 128, D])
    flag_d = din("flag", [128, 1])
    spool_d = din("spool", [DEPTH, 60, 512])
    sconv_d = din("sconv", [DEPTH, 120, 768])
    pmat_d = din("pmat", [128, 20, 128])
    ident_d = din("ident", [128, 128])
    w_in_d = din("w_in", [DEPTH, D, INW])
    w_out_d = din("w_out", [DEPTH, D, D])
    w_gate_d = din("w_gate", [DEPTH, D, DFF])
    w_up_d = din("w_up", [DEPTH, D, DFF])
    w_down_d = din("w_down", [DEPTH, DFF, D])
    g4_d = din("g4", [8, D])
    pool_w_d = din("pool_w", [DEPTH, 4, 128, 128])
    vecs_d = din("vecs", [DEPTH, 4, 768])
    sgu_norm_d = din("sgu_norm", [DEPTH, 768])
    sgu_ws_d = din("sgu_ws", [DEPTH, 6, 128, 128])
    sgu_b_d = din("sgu_b", [DEPTH, 6, 128])
    conv_w_d = din("conv_w", [DEPTH, 31, 768])

    yp_d = dout("yp", [16, 128, D])
    ys_d = dout("ys", [128, D])
    npool_p_d = dout("npool_p", [DEPTH, 128, 512])
    nconv_p_d = dout("nconv_p", [DEPTH, 128, 768])
    npool_s_d = dout("npool_s", [DEPTH, 128, 512])
    nconv_s_d = dout("nconv_s", [DEPTH, 128, 768])
    nv_s_d = dout("nv_s", [DEPTH, 128, 768])

    def sb(name, shape, dt):
        return nc.alloc_sbuf_tensor(name, list(shape), dt)

    xbuf = sb("xbuf", [128, 4, D], F32)
    blkB = sb("blkB", [128, 8192], F32)
    A_WORDS = 14852 + 512
    blkA = sb("blkA", [128, A_WORDS], F32)
    wring = [sb(f"wring{i}", [128, SLOTW], BF16) for i in range(NSLOT)]
    hb = sb("hb", [128, D], BF16)
    junk = sb("junk", [128, D], BF16)
    gbc = sb("gbc", [128, D], F32)
    idf = sb("idf", [128, 128], F32)
    idb = sb("idb", [128, 128], BF16)
    pm_b = sb("pm_b", [128, 20, 128], BF16)
    flag = sb("flagt", [128, 1], F32)
    poolw_b = sb("poolw_b", [128, 2, 4, 128], BF16)
    wmT = sb("wmT", [128, 2, 2, 6, 128], BF16)
    brow = sb("brow", [1, 2, 6, 128], BF16)
    ones_row = sb("ones_row", [1, 128], BF16)
    convwT = sb("convwT", [128, 2, 6, 31], F32)
    vecT = sb("vecT", [128, 2, 6, 4], F32)
    gT = sb("gT", [128, 8, 16], F32)
    vbc = sb("vbc", [128, 768], F32)
    xa_prev = sb("xa_prev", [128, 2, 512], BF16)
    cext = sb("cext", [128, 2, 6, 30], F32)
    sp_b = sb("sp_b", [128, 2, 512], BF16)
    stat = sb("stat", [128, 64], F32)
    epsc = sb("epsc", [128, 2], F32)
    ps = nc.alloc_psum_tensor("ps", [128, 8, 512], F32)

    hT = blkB[:, 0:4096].bitcast(BF16).rearrange("p (k n) -> p k n", k=16)
    mixT = blkB[:, 4096:8192].bitcast(BF16).rearrange("p (k n) -> p k n", k=16)
    f_fm = blkB[:, 0:8192].rearrange("p (k n) -> p k n", k=16)
    hid = blkA[:, 0:11264].bitcast(BF16).rearrange("p (k n) -> p k n", k=44)
    m_fm = blkA[:, 0:8192].rearrange("p (k n) -> p k n", k=16)
    o = 0
    ext = blkA[:, o:o + 3252].rearrange("p (c n) -> p c n", c=6); o += 3252
    ext_s = blkA[:, o:o + 1488].rearrange("p (c b n) -> p c b n", c=6, b=4); o += 1488
    ycv = blkA[:, o:o + 3072].rearrange("p (c n) -> p c n", c=6); o += 3072
    xa_b = blkA[:, o:o + 1024].bitcast(BF16).rearrange("p (i n) -> p i n", i=4); o += 1024
    v_b = blkA[:, o:o + 1536].bitcast(BF16).rearrange("p (i n) -> p i n", i=4); o += 1536
    xa_f = blkA[:, o:o + 512]; o += 512
    v_f = blkA[:, o:o + 768]; o += 768
    hs_tok = blkA[:, o:o + 768]; o += 768
    u_sb = blkA[:, o:o + 512]; o += 512
    sg = blkA[:, o:o + 512]; o += 512
    yn_b = blkA[:, o:o + 384].bitcast(BF16); o += 384
    pg_b = blkA[:, o:o + 256].bitcast(BF16); o += 256
    sc_nat = blkA[:, o:o + 768]; o += 768
    xa_f2 = blkA[:, o:o + 512]; o += 512
    assert o <= A_WORDS, o
    hs_fm = gbc[:, 0:768].rearrange("p (c n) -> p c n", c=6)
    pm_f = blkA[:, 0:2560].rearrange("p (a n) -> p a n", a=20)
    ws_n = blkA[:, 2560:2560 + 768].rearrange("p (h n) -> p h n", h=6)
    ws_s = blkA[:, 3328:3328 + 768].rearrange("p (h n) -> p h n", h=6)
    pw_f = blkA[:, 4096:4096 + 512].rearrange("p (g n) -> p g n", g=4)
    cw_n = blkA[:, 4608:4608 + 768]
    vc_n = blkA[:, 5376:5376 + 768]
    g_n = blkA[:, 6144:6144 + 128]
    sb_f = blkA[:, 6272:6272 + 1536].rearrange("p (l h n) -> p l h n", l=2, h=6)
    sp_f = blkA[:, 7808:7808 + 1024].rearrange("p (l n) -> p l n", l=2)

    stg = ps[:, 4:8, :].rearrange("p a b -> p (a b)")

    def psb(b):
        return ps[:, b, :].bitcast(BF16)

    stack = ExitStack()
    S = Sched(nc, stack)
    accstate = {"i": 0}

    def acc():
        b = accstate["i"]
        accstate["i"] = (b + 1) % 4
        return b

    def acc2():
        b = 0 if accstate["i"] in (0, 3) else 2
        accstate["i"] = (b + 2) % 4
        return b

    PSK = lambda b: ("ps", b)
    STGK = [("ps", 4), ("ps", 5), ("ps", 6), ("ps", 7)]
    stat_col = {"i": 0}

    def scol(n=1):
        c = stat_col["i"]
        if c + n > 64:
            c = 0
        stat_col["i"] = c + n
        return c

    def dma1(out_ap, in_ap, **kw):
        return lambda e: [e.dma_start(out=out_ap, in_=in_ap, **kw)]

    S.op("sp", dma1(idf[:], ident_d[:, :]), writes=["idf"], dma=1)
    S.op("sp", dma1(pm_f, pmat_d[:, :, :]), writes=["pm_f"], dma=1)
    S.op("sp", dma1(flag[:], flag_d[:, :]), writes=["flag"], dma=1)
    S.op("dve", lambda e: e.tensor_copy(out=idb[:], in_=idf[:]), reads=["idf"], writes=["idb"])
    S.op("dve", lambda e: e.tensor_copy(out=pm_b[:], in_=pm_f), reads=["pm_f"], writes=["pm_b"])
    S.op("dve", lambda e: e.memset(ones_row[:], 1.0), writes=["ones_row"])
    S.op("dve", lambda e: e.memset(epsc[:, 0:1], RMS_EPS), writes=["epsc0"])
    S.op("dve", lambda e: e.memset(epsc[:, 1:2], LN_EPS), writes=["epsc1"])
    S.op("dve", lambda e: e.memset(cext[:], 0.0), writes=[("cext", 0), ("cext", 1)])
    S.op("dve", lambda e: e.memset(xa_prev[:], 0.0), writes=[("xa_prev", 0), ("xa_prev", 1)])
    S.op("sp", dma1(g_n, g4_d.rearrange("n (kc p) -> (n kc) p", p=128)), writes=["g_n"], dma=1)
    b = acc()
    S.op("pe", lambda e, b=b: e.transpose(ps[:, b, 0:128], g_n, idf[:]), reads=["g_n", "idf"], writes=[PSK(b)])
    S.op("dve", lambda e, b=b: e.tensor_copy(out=gT[:].rearrange("p a b -> p (a b)"), in_=ps[:, b, 0:128]),
         reads=[PSK(b)], writes=["gT"])
    S.op("sp", dma1(sb_f[0:1], sgu_b_d.rearrange("(o l) h n -> o l h n", o=1)), writes=["sb_f"], dma=1)
    S.op("sp", dma1(sp_f[0:60], spool_d.rearrange("l r n -> r l n")), writes=["sp_f"], dma=1)
    S.op("dve", lambda e: e.memset(sp_b[:], 0.0), writes=["sp_b"])
    S.op("dve", lambda e: e.tensor_copy(out=sp_b[0:60], in_=sp_f[0:60]), reads=["sp_f", "sp_b"], writes=["sp_b"])
    for l in range(DEPTH):
        S.op("sp", dma1(pw_f, pool_w_d[l].rearrange("g c d -> c g d")), writes=["pw_f"], dma=1)
        S.op("dve", lambda e, l=l: e.tensor_copy(out=poolw_b[:, l], in_=pw_f), reads=["pw_f"], writes=[("poolw", l)])
        S.op("sp", dma1(ws_n, sgu_ws_d[l].rearrange("h i j -> i h j")), writes=["ws_n"], dma=1)
        S.op("dve", lambda e: e.memset(ws_n[0:64, :, 64:128], 0.0), reads=[], writes=["ws_n"])
        S.op("dve", lambda e: e.memset(ws_s, 0.0), writes=["ws_s"])

        def ld_ws_s(e, l=l):
            r = []
            for bb in range(4):
                r.append(e.dma_start(out=ws_s[32 * bb:32 * bb + 32, :, 32 * bb:32 * bb + 32],
                                     in_=sgu_ws_d[l, :, 0:32, 0:32].rearrange("h i j -> i h j")))
            return r
        S.op("sp", ld_ws_s, writes=["ws_s"], dma=4)
        for kind, src, key in ((0, ws_n, "ws_n"), (1, ws_s, "ws_s")):
            b0 = acc2()

            def tr6(e, src=src, b0=b0):
                for h in range(6):
                    r = e.transpose(ps[:, b0 + h // 4, (h % 4) * 128:(h % 4 + 1) * 128], src[:, h, :], idf[:])
                return r
            S.op("pe", tr6, reads=[key, "idf"], writes=[PSK(b0), PSK(b0 + 1)])
            S.op("dve", lambda e, l=l, kind=kind, b0=b0: e.tensor_copy(
                out=wmT[:, l, kind].rearrange("p h n -> p (h n)"),
                in_=ps[:, b0:b0 + 2, :].rearrange("p a b -> p (a b)")[:, 0:768]),
                reads=[PSK(b0), PSK(b0 + 1)], writes=[("wmT", l, kind)])
        S.op("dve", lambda e, l=l: e.tensor_copy(out=brow[0:1, l], in_=sb_f[0:1, l]), reads=["sb_f"], writes=[("brow", l)])
        S.op("sp", dma1(cw_n[0:31], conv_w_d[l]), writes=["cw_n"], dma=1)
        b = acc()

        def trcw(e, b=b):
            for c in range(6):
                r = e.transpose(ps[:, b, c * 32:c * 32 + 31], cw_n[0:31, c * 128:(c + 1) * 128], idf[0:31, 0:31])
            return r
        S.op("pe", trcw, reads=["cw_n", "idf"], writes=[PSK(b)])
        S.op("dve", lambda e, l=l, b=b: e.tensor_copy(out=convwT[:, l], in_=ps[:, b, 0:192].rearrange("p (c k) -> p c k", c=6)[:, :, 0:31]),
             reads=[PSK(b)], writes=[("convwT", l)])
        S.op("sp", dma1(vc_n[0:4], vecs_d[l]), writes=["vc_n"], dma=1)
        b = acc()

        def trvc(e, b=b):
            for c in range(6):
                r = e.transpose(ps[:, b, c * 4:c * 4 + 4], vc_n[0:4, c * 128:(c + 1) * 128], idf[0:4, 0:4])
            return r
        S.op("pe", trvc, reads=["vc_n", "idf"], writes=[PSK(b)])
        S.op("dve", lambda e, l=l, b=b: e.tensor_copy(out=vecT[:, l].rearrange("p c k -> p (c k)"), in_=ps[:, b, 0:24]),
             reads=[PSK(b)], writes=[("vecT", l)])

    SETUP_KEYS = ["pm_f", "ws_n", "ws_s", "pw_f", "cw_n", "vc_n", "g_n", "sb_f", "sp_f"]

    wstate = {"i": 0}

    def wload(parts):
        s = wstate["i"]
        wstate["i"] = (s + 1) % NSLOT
        slot = wring[s]

        def fn(e, slot=slot, parts=parts):
            return [e.dma_start(out=dv(slot), in_=src) for dv, src in parts]
        S.op("pool", fn, writes=[("w", s)], dma=len(parts), wslot=s)
        return slot, ("w", s)

    def wtile_cols(wd_l, c0, w):
        src = wd_l.rearrange("(kc p) n -> p kc n", p=128)[:, :, c0:c0 + w]
        return [(lambda slot, w=w: slot[:, 0:16 * w].rearrange("p (kc n) -> p kc n", kc=16), src)]

    def slot_view(slot, w):
        return slot[:, 0:16 * w].rearrange("p (kc n) -> p kc n", kc=16)

    def rstd_chain(ssq_key, ssq_ap, n, eps_col, out_ap, out_key):
        S.op("act", lambda e: e.activation(out=out_ap, in_=ssq_ap, func=AF.Sqrt, bias=epsc[:, eps_col:eps_col + 1], scale=1.0 / n),
             reads=[ssq_key, "epsc0", "epsc1"], writes=[out_key], selfsync=True)
        S.op("dve", lambda e: e.reciprocal(out=out_ap, in_=out_ap), reads=[out_key], writes=[out_key])

    def prenorm_to_hT(tiles, gidx):
        for i, t in enumerate(tiles):
            c = scol(2)
            kq, kr = ("st", c), ("st", c + 1)
            S.op("act", lambda e, i=i, c=c: e.activation(out=junk[:], in_=xbuf[:, i, :], func=AF.Square, accum_out=stat[:, c:c + 1]),
                 reads=[("x", i)], writes=["junk", kq])
            rstd_chain(kq, stat[:, c:c + 1], D, 0, stat[:, c + 1:c + 2], kr)
            S.op("act", lambda e, i=i, c=c: e.activation(out=hb[:], in_=xbuf[:, i, :], func=AF.Identity, scale=stat[:, c + 1:c + 2]),
                 reads=[("x", i), kr], writes=["hb"])
            for half in range(2):
                b = acc()

                def tr8(e, half=half, b=b):
                    for q in range(8):
                        kc = half * 8 + q
                        r = e.transpose(psb(b)[:, q * 128:(q + 1) * 128], hb[:, kc * 128:(kc + 1) * 128], idb[:])
                    return r
                S.op("pe", tr8, reads=["hb", "idb"], writes=[PSK(b)])
                gsl = gT[:, gidx, half * 8:half * 8 + 8]
                S.op("dve", lambda e, i=i, half=half, b=b, gsl=gsl: e.tensor_tensor(
                    out=hT[:, half * 8:half * 8 + 8, i * 128:(i + 1) * 128],
                    in0=psb(b).rearrange("p (k n) -> p k n", k=8),
                    in1=gsl.unsqueeze(2).broadcast_to([128, 8, 128]), op=ALU.mult),
                    reads=[PSK(b), "gT"], writes=[("hT", i, half)])

    def postnorm_residual(tiles, src_fm, src_keyname, gidx, l):
        S.alias([("hs_fm", c) for c in range(6)], ["gbc"])
        S.op("sp", dma1(gbc[:], bass.AP(g4_d.tensor, gidx * D, [[0, 128], [1, D]])), writes=["gbc"], dma=1)
        for i, t in enumerate(tiles):
            def tr16(e, i=i):
                for j in range(16):
                    r = e.transpose(stg[:, j * 128:(j + 1) * 128], src_fm[:, j, i * 128:(i + 1) * 128], idf[:])
                return r
            S.op("pe", tr16, reads=[(src_keyname, j, i) for j in range(16)] + ["idf"], writes=STGK)
            c = scol(2)
            kq, kr = ("st", c), ("st", c + 1)
            S.op("act", lambda e, c=c: e.activation(out=junk[:], in_=stg, func=AF.Square, accum_out=stat[:, c:c + 1]),
                 reads=STGK, writes=["junk", kq])
            rstd_chain(kq, stat[:, c:c + 1], D, 0, stat[:, c + 1:c + 2], kr)
            tmpv = src_fm[:, :, i * 128:(i + 1) * 128]
            S.op("dve", lambda e, c=c, tmpv=tmpv: e.scalar_tensor_tensor(
                out=tmpv, in0=stg.rearrange("p (k n) -> p k n", k=16), scalar=stat[:, c + 1:c + 2],
                in1=gbc[:].rearrange("p (k n) -> p k n", k=16), op0=ALU.mult, op1=ALU.mult),
                reads=STGK + [kr, "gbc"], writes=[(src_keyname, j, i) for j in range(16)])
            S.op("dve", lambda e, i=i, tmpv=tmpv: e.tensor_tensor(
                out=xbuf[:, i, :].rearrange("p (k n) -> p k n", k=16), in0=xbuf[:, i, :].rearrange("p (k n) -> p k n", k=16),
                in1=tmpv, op=ALU.add),
                reads=[(src_keyname, j, i) for j in range(16)] + [("x", i)], writes=[("x", i)])

    MIXTMP_KEYS = ([("ext", c) for c in range(6)] + [("exts", c) for c in range(6)] + [("ycv", c) for c in range(6)] + [("ycvs", c) for c in range(6)]
                   + [("xa_b", i) for i in range(4)] + [("v_b", i) for i in range(4)]
                   + ["xa_f", "xa_f2", "v_f", "hs_tok", "u_sb", "sg", "yn_b", "pg_b", "sc_nat"])
    MFM_KEYS = [("mfm", j, i) for j in range(16) for i in range(4)]
    HID_KEYS = [("hid", j) for j in range(NFC)]
    FFM_KEYS = [("ffm", j, i) for j in range(16) for i in range(4)]
    BACT_KEYS = [("hT", i, h) for i in range(4) for h in range(2)] + [("mixT", c, i) for c in range(16) for i in range(4)]

    def layer(gi, tiles, l):
        nt = len(tiles)
        N = 128 * nt
        has_s = SAMPLE_T in tiles
        npt = nt - (1 if has_s else 0)
        Np = 128 * npt
        first = (gi == 0)
        w_in_l, w_out_l, w_gate_l, w_up_l, w_down_l = w_in_d[l], w_out_d[l], w_gate_d[l], w_up_d[l], w_down_d[l]
        hT_all = [("hT", i, h) for i in range(nt) for h in range(2)]

        S.new_phase()
        S.alias(FFM_KEYS, BACT_KEYS)
        S.alias(HID_KEYS + MFM_KEYS + SETUP_KEYS, MIXTMP_KEYS)
        S.alias(["gbc"], [("hs_fm", c) for c in range(6)])

        stage(1)
        S.op("sp", dma1(vbc[:], bass.AP(sgu_norm_d.tensor, l * 768, [[0, 128], [1, 768]])), writes=["vbc"], dma=1)
        prenorm_to_hT(tiles, l * 4 + 0)

        pending = []

        def pend(fn, reads=(), writes=()):
            pending.append((fn, reads, writes))

        def drain(n):
            for _ in range(min(n, len(pending))):
                fn, r, w = pending.pop(0)
                S.op("dve", fn, reads=r, writes=w)

        def Sdve(fn, reads=(), writes=(), **kw):
            ev = S.op("dve", fn, reads=reads, writes=writes, **kw)
            drain(10)
            return ev

        stage(5)
        S.op("dve", lambda e: e.tensor_copy(out=ext[:, :, 0:30], in_=cext[:, l]), reads=[("cext", l)], writes=[("ext", c) for c in range(6)])
        if has_s:
            S.op("sp", dma1(sc_nat[0:120], sconv_d[l]), writes=["sc_nat"], dma=1)
            for c in range(6):
                b = acc()
                S.op("pe", lambda e, c=c, b=b: e.transpose(ps[:, b, 0:120], sc_nat[0:120, c * 128:(c + 1) * 128], idf[0:120, 0:120]),
                     reads=["sc_nat", "idf"], writes=[PSK(b)])
                S.op("dve", lambda e, c=c, b=b: e.tensor_copy(out=ext_s[:, c, :, 0:30], in_=ps[:, b, 0:120].rearrange("p (b r) -> p b r", b=4)),
                     reads=[PSK(b)], writes=[("exts", c)])
        for c in range(6):
            slot, wk = wload([
                (lambda s: slot_view(s, 256)[:, :, 0:128], w_in_l.rearrange("(kc p) n -> p kc n", p=128)[:, :, 2048 + 128 * c:2048 + 128 * c + 128]),
                (lambda s: slot_view(s, 256)[:, :, 128:256], w_in_l.rearrange("(kc p) n -> p kc n", p=128)[:, :, 2816 + 128 * c:2816 + 128 * c + 128]),
            ])
            bA, bG = acc(), acc()

            def mm_cc(e, slot=slot, bA=bA, bG=bG):
                sv = slot_view(slot, 256)
                for bb, c0 in ((bA, 0), (bG, 128)):
                    for kc in range(16):
                        r = e.matmul(ps[:, bb, 0:N], sv[:, kc, c0:c0 + 128], hT[:, kc, 0:N], start=(kc == 0), stop=(kc == 15))
                return r
            S.op("pe", mm_cc, reads=hT_all + [wk], writes=[PSK(bA), PSK(bG)])
            S.op("act", lambda e, bG=bG: e.activation(out=sg[:, 0:N], in_=ps[:, bG, 0:N], func=AF.Sigmoid), reads=[PSK(bG)], writes=["sg"])
            S.op("dve", lambda e, c=c, bA=bA: e.tensor_tensor(out=ext[:, c, 30:30 + Np], in0=ps[:, bA, 0:Np], in1=sg[:, 0:Np], op=ALU.mult),
                 reads=[PSK(bA), "sg"], writes=[("ext", c)])
            if has_s:
                S.op("dve", lambda e, c=c, bA=bA: e.tensor_tensor(out=hs_fm[:, c, :], in0=ps[:, bA, Np:N], in1=sg[:, Np:N], op=ALU.mult),
                     reads=[PSK(bA), "sg"], writes=[("hs_fm", c)])
                S.op("dve", lambda e, c=c: e.tensor_copy(out=ext_s[:, c, :, 30:62], in_=hs_fm[:, c, :].rearrange("p (b j) -> p b j", b=4)),
                     reads=[("hs_fm", c)], writes=[("exts", c)])
            if first:
                S.op("dve", lambda e, c=c: e.tensor_scalar(out=ext[:, c, 30:158], in0=ext[:, c, 30:158], scalar1=flag[:, 0:1], scalar2=None, op0=ALU.mult),
                     reads=[("ext", c), "flag"], writes=[("ext", c)])

            ys_ = ycv[:, c, Np:N].rearrange("p (b j) -> p b j", b=4) if has_s else None
            for k in range(31):
                pass
            for k in range(31):
                if k == 0:
                    pend(lambda e, c=c: e.tensor_scalar(out=ycv[:, c, 0:Np], in0=ext[:, c, 0:Np], scalar1=convwT[:, l, c, 0:1],
                                                               scalar2=vecT[:, l, c, 1:2], op0=ALU.mult, op1=ALU.add),
                         reads=[("ext", c), ("convwT", l), ("vecT", l)], writes=[("ycv", c)])
                    if has_s:
                        pend(lambda e, c=c, ys_=ys_: e.tensor_scalar(out=ys_, in0=ext_s[:, c, :, 0:32], scalar1=convwT[:, l, c, 0:1],
                                                                            scalar2=vecT[:, l, c, 1:2], op0=ALU.mult, op1=ALU.add),
                             reads=[("exts", c), ("convwT", l), ("vecT", l)], writes=[("ycvs", c)])
                else:
                    pend(lambda e, c=c, k=k: e.scalar_tensor_tensor(out=ycv[:, c, 0:Np], in0=ext[:, c, k:k + Np], scalar=convwT[:, l, c, k:k + 1],
                                                                           in1=ycv[:, c, 0:Np], op0=ALU.mult, op1=ALU.add),
                         reads=[("ext", c), ("ycv", c)], writes=[("ycv", c)])
                    if has_s:
                        pend(lambda e, c=c, k=k, ys_=ys_: e.scalar_tensor_tensor(out=ys_, in0=ext_s[:, c, :, k:k + 32], scalar=convwT[:, l, c, k:k + 1],
                                                                                        in1=ys_, op0=ALU.mult, op1=ALU.add),
                             reads=[("exts", c), ("ycvs", c)], writes=[("ycvs", c)])
        S.op("dve", lambda e: e.tensor_copy(out=cext[:, l], in_=ext[:, :, Np:Np + 30]), reads=[("ext", c) for c in range(6)], writes=[("cext", l)])
        if LASTP_T in tiles:
            ip = tiles.index(LASTP_T)

            def tr_hp(e, ip=ip):
                for c in range(6):
                    r = e.transpose(stg[:, c * 128:(c + 1) * 128], ext[:, c, 30 + ip * 128:30 + (ip + 1) * 128], idf[:])
                return r
            S.op("pe", tr_hp, reads=[("ext", c) for c in range(6)] + ["idf"], writes=STGK[0:2])
            S.op("dve", lambda e: e.tensor_copy(out=hs_tok, in_=stg[:, 0:768]), reads=STGK[0:2], writes=["hs_tok"])
            S.op("sp", dma1(nconv_p_d[l], hs_tok), reads=["hs_tok"], dma=1)
        if has_s:
            def tr_hs(e):
                for c in range(6):
                    r = e.transpose(stg[:, c * 128:(c + 1) * 128], hs_fm[:, c, :], idf[:])
                return r
            S.op("pe", tr_hs, reads=[("hs_fm", c) for c in range(6)] + ["idf"], writes=STGK[0:2])
            S.op("dve", lambda e: e.tensor_copy(out=hs_tok, in_=stg[:, 0:768]), reads=STGK[0:2], writes=["hs_tok"])

            S.op("sp", dma1(nconv_s_d[l], hs_tok), reads=["hs_tok"], dma=1)

        stage(2)
        sl0, k0 = wload(wtile_cols(w_in_l, 0, 256))
        sl1, k1 = wload(wtile_cols(w_in_l, 256, 256))
        for i, t in enumerate(tiles):
            b = acc()

            def mm_xa(e, i=i, b=b):
                for half, sl in ((0, sl0), (1, sl1)):
                    sv = slot_view(sl, 256)
                    for kc in range(16):
                        r = e.matmul(ps[:, b, half * 256:(half + 1) * 256], hT[:, kc, i * 128:(i + 1) * 128], sv[:, kc, :],
                                     start=(kc == 0), stop=(kc == 15))
                return r
            S.op("pe", mm_xa, reads=[("hT", i, 0), ("hT", i, 1), k0, k1], writes=[PSK(b)])
            if t == HALO_T:
                Sdve(lambda e, i=i, b=b: e.tensor_scalar(out=xa_b[:, i, :], in0=ps[:, b, :], scalar1=flag[:, 0:1], scalar2=None, op0=ALU.mult),
                     reads=[PSK(b), "flag"], writes=[("xa_b", i)])
            else:
                Sdve(lambda e, i=i, b=b: e.tensor_copy(out=xa_b[:, i, :], in_=ps[:, b, :]), reads=[PSK(b)], writes=[("xa_b", i)])
            if t in (LASTP_T, SAMPLE_T) and not ('noxaf16' in _DBG and t == LASTP_T) and not ('noxaf17' in _DBG and t == SAMPLE_T):
                xaf, xak = (xa_f, "xa_f") if t == LASTP_T else (xa_f2, "xa_f2")
                Sdve(lambda e, b=b, xaf=xaf: e.tensor_copy(out=xaf, in_=ps[:, b, :]), reads=[PSK(b)], writes=[xak])
                if 'nopoolst' not in _DBG:
                    S.op("sp", dma1(npool_p_d[l] if t == LASTP_T else npool_s_d[l], xaf), reads=[xak], dma=1)
            if t == SAMPLE_T:
                cur, prv, K2 = 12, 16, 128
                prev_ap = lambda g: sp_b[:, l, g * 128:(g + 1) * 128]
                prev_key = "sp_b"
                if 'genmat' in _DBG:
                    cur, prv = 0, 4
                if 'noprevs' in _DBG:
                    prev_ap = lambda g: xa_prev[:, l, g * 128:(g + 1) * 128]
                    prev_key = ("xa_prev", l)
            else:
                cur = 8 if t == 1 else 0
                prv, K2 = 4, 128
                if i == 0:
                    prev_ap = lambda g: xa_prev[:, l, g * 128:(g + 1) * 128]
                    prev_key = ("xa_prev", l)
                else:
                    prev_ap = lambda g, i=i: xa_b[:, i - 1, g * 128:(g + 1) * 128]
                    prev_key = ("xa_b", i - 1)
            b2 = acc()

            def mm_pool1(e, i=i, b2=b2, cur=cur, prv=prv, K2=K2, prev_ap=prev_ap):
                for g in range(4):
                    e.matmul(ps[:, b2, g * 128:(g + 1) * 128], xa_b[:, i, g * 128:(g + 1) * 128], pm_b[:, cur + g, :], start=True, stop=False)
                    r = e.matmul(ps[:, b2, g * 128:(g + 1) * 128], prev_ap(g), pm_b[0:K2, prv + g, :], start=False, stop=True)
                return r
            S.op("pe", mm_pool1, reads=[("xa_b", i), prev_key, "pm_b"], writes=[PSK(b2)])
            Sdve(lambda e, b2=b2: e.tensor_copy(out=pg_b, in_=ps[:, b2, :]), reads=[PSK(b2)], writes=["pg_b"])
            b3 = acc()

            def mm_pool2(e, b3=b3):
                for g in range(4):
                    r = e.matmul(ps[:, b3, g * 128:(g + 1) * 128], poolw_b[:, l, g, :], pg_b[:, g * 128:(g + 1) * 128], start=True, stop=True)
                return r
            S.op("pe", mm_pool2, reads=["pg_b", ("poolw", l)], writes=[PSK(b3)])

            def ev_pool(e, i=i, b3=b3):
                for g in range(4):
                    r = e.activation(out=mixT[:, g, i * 128:(i + 1) * 128], in_=ps[:, b3, g * 128:(g + 1) * 128], func=AF.Identity,
                                     scale=vecT[:, l, g, 0:1])
                return r
            S.op("act", ev_pool, reads=[PSK(b3), ("vecT", l)], writes=[("mixT", g, i) for g in range(4)])
        Sdve(lambda e: e.tensor_copy(out=xa_prev[:, l, :], in_=xa_b[:, npt - 1, :]), reads=[("xa_b", npt - 1)], writes=[("xa_prev", l)])

        stage(3)
        slv = [wload(wtile_cols(w_in_l, 1280 + 256 * q, 256)) for q in range(3)]
        for i, t in enumerate(tiles):
            b0 = acc2()
            pv = ps[:, b0:b0 + 2, :].rearrange("p a b -> p (a b)")

            def mm_v(e, i=i, pv=pv):
                for q in range(3):
                    sv = slot_view(slv[q][0], 256)
                    for kc in range(16):
                        r = e.matmul(pv[:, q * 256:(q + 1) * 256], hT[:, kc, i * 128:(i + 1) * 128], sv[:, kc, :], start=(kc == 0), stop=(kc == 15))
                return r
            S.op("pe", mm_v, reads=[("hT", i, 0), ("hT", i, 1)] + [k for _, k in slv], writes=[PSK(b0), PSK(b0 + 1)])
            c = scol(2)
            kq, kr = ("st", c), ("st", c + 1)
            S.op("act", lambda e, pv=pv, c=c: e.activation(out=junk[:, 0:768], in_=pv[:, 0:768], func=AF.Square, accum_out=stat[:, c:c + 1]),
                 reads=[PSK(b0), PSK(b0 + 1)], writes=["junk", kq])
            rstd_chain(kq, stat[:, c:c + 1], 768, 0, stat[:, c + 1:c + 2], kr)
            if t == SAMPLE_T:
                Sdve(lambda e, pv=pv, c=c: e.scalar_tensor_tensor(out=v_f, in0=pv[:, 0:768], scalar=stat[:, c + 1:c + 2], in1=vbc[:],
                                                                         op0=ALU.mult, op1=ALU.mult),
                     reads=[PSK(b0), PSK(b0 + 1), kr, "vbc"], writes=["v_f"])
                Sdve(lambda e, i=i: e.tensor_copy(out=v_b[:, i, :], in_=v_f), reads=["v_f"], writes=[("v_b", i)])
                S.op("sp", dma1(nv_s_d[l], v_f), reads=["v_f"], dma=1)
            else:
                Sdve(lambda e, pv=pv, c=c, i=i: e.scalar_tensor_tensor(out=v_b[:, i, :], in0=pv[:, 0:768], scalar=stat[:, c + 1:c + 2],
                                                                              in1=vbc[:], op0=ALU.mult, op1=ALU.mult),
                     reads=[PSK(b0), PSK(b0 + 1), kr, "vbc"], writes=[("v_b", i)])

        stage(4)
        slu = [wload(wtile_cols(w_in_l, 512 + 256 * q, 256)) for q in range(3)]
        for h in range(6):
            slot, wk = slu[h // 2]
            c0 = (h % 2) * 128
            bU = acc()

            def mm_u(e, slot=slot, c0=c0, bU=bU):
                sv = slot_view(slot, 256)
                for kc in range(16):
                    r = e.matmul(ps[:, bU, 0:N], sv[:, kc, c0:c0 + 128], hT[:, kc, 0:N], start=(kc == 0), stop=(kc == 15))
                return r
            S.op("pe", mm_u, reads=hT_all + [wk], writes=[PSK(bU)])
            bM = acc()

            def mm_sgu(e, h=h, bM=bM):
                for i, t in enumerate(tiles):
                    if t == SAMPLE_T:
                        e.matmul(ps[:, bM, i * 128:(i + 1) * 128], v_b[:, i, h * 128:(h + 1) * 128], wmT[:, l, 1, h, :], start=True, stop=False)
                        for bb in range(4):
                            r = e.matmul(ps[:, bM, i * 128 + 32 * bb:i * 128 + 32 * bb + 32], ones_row[0:1, :], brow[0:1, l, h, 0:32],
                                         start=False, stop=(bb == 3))
                    else:
                        e.matmul(ps[:, bM, i * 128:(i + 1) * 128], v_b[:, i, h * 128:(h + 1) * 128], wmT[:, l, 0, h, :], start=True, stop=False)
                        r = e.matmul(ps[:, bM, i * 128:(i + 1) * 128], ones_row[0:1, :], brow[0:1, l, h, :], start=False, stop=True)
                return r
            S.op("pe", mm_sgu, reads=[("v_b", i) for i in range(nt)] + [("wmT", l, 0), ("wmT", l, 1), ("brow", l), "ones_row"], writes=[PSK(bM)])
            S.op("act", lambda e, bU=bU: e.activation(out=u_sb[:, 0:N], in_=ps[:, bU, 0:N], func=AF.Copy), reads=[PSK(bU)], writes=["u_sb"])
            Sdve(lambda e, h=h, bM=bM: e.tensor_tensor(out=mixT[:, 4 + h, 0:N], in0=u_sb[:, 0:N], in1=ps[:, bM, 0:N], op=ALU.mult),
                 reads=["u_sb", PSK(bM)], writes=[("mixT", 4 + h, i) for i in range(nt)])

        drain(len(pending))

        stage(6)
        for i, t in enumerate(tiles):
            def tr_y(e, i=i):
                for c in range(6):
                    r = e.transpose(stg[:, c * 128:(c + 1) * 128], ycv[:, c, i * 128:(i + 1) * 128], idf[:])
                return r
            S.op("pe", tr_y, reads=[("ycv", c) for c in range(6)] + [("ycvs", c) for c in range(6)] + ["idf"], writes=STGK[0:2])
            c0 = scol(6)
            k1_, k2_ = ("st", c0), ("st", c0 + 1)
            S.op("act", lambda e, c0=c0: e.activation(out=junk[:, 0:768], in_=stg[:, 0:768], func=AF.Identity, accum_out=stat[:, c0:c0 + 1]),
                 reads=STGK[0:2], writes=["junk", k1_])
            S.op("act", lambda e, c0=c0: e.activation(out=junk[:, 0:768], in_=stg[:, 0:768], func=AF.Square, accum_out=stat[:, c0 + 1:c0 + 2]),
                 reads=STGK[0:2], writes=["junk", k2_])
            km, kv, kr, kn = ("st", c0 + 2), ("st", c0 + 3), ("st", c0 + 4), ("st", c0 + 5)
            S.op("dve", lambda e, c0=c0: e.tensor_scalar(out=stat[:, c0 + 2:c0 + 3], in0=stat[:, c0:c0 + 1], scalar1=1.0 / 768, scalar2=None, op0=ALU.mult),
                 reads=[k1_], writes=[km])
            S.op("dve", lambda e, c0=c0: e.scalar_tensor_tensor(out=stat[:, c0 + 3:c0 + 4], in0=stat[:, c0 + 2:c0 + 3], scalar=-1.0, in1=stat[:, c0 + 2:c0 + 3],
                                                                op0=ALU.mult, op1=ALU.mult),
                 reads=[km], writes=[kv], selfsync=True)
            S.op("dve", lambda e, c0=c0: e.scalar_tensor_tensor(out=stat[:, c0 + 3:c0 + 4], in0=stat[:, c0 + 1:c0 + 2], scalar=1.0 / 768, in1=stat[:, c0 + 3:c0 + 4],
                                                                op0=ALU.mult, op1=ALU.add),
                 reads=[k2_, kv], writes=[kv], selfsync=True)
            S.op("act", lambda e, c0=c0: e.activation(out=stat[:, c0 + 4:c0 + 5], in_=stat[:, c0 + 3:c0 + 4], func=AF.Sqrt, bias=epsc[:, 1:2], scale=1.0),
                 reads=[kv, "epsc1"], writes=[kr])
            S.op("dve", lambda e, c0=c0: e.reciprocal(out=stat[:, c0 + 4:c0 + 5], in_=stat[:, c0 + 4:c0 + 5]), reads=[kr], writes=[kr])
            S.op("dve", lambda e, c0=c0: e.scalar_tensor_tensor(out=stat[:, c0 + 5:c0 + 6], in0=stat[:, c0 + 2:c0 + 3], scalar=-1.0, in1=stat[:, c0 + 4:c0 + 5],
                                                                op0=ALU.mult, op1=ALU.mult),
                 reads=[km, kr], writes=[kn], selfsync=True)
            S.op("dve", lambda e, c0=c0: e.tensor_scalar(out=yn_b, in0=stg[:, 0:768], scalar1=stat[:, c0 + 4:c0 + 5], scalar2=stat[:, c0 + 5:c0 + 6],
                                                         op0=ALU.mult, op1=ALU.add),
                 reads=STGK[0:2] + [kr, kn], writes=["yn_b"], selfsync=True)
            b = acc()

            def tr_yn(e, b=b):
                for c in range(6):
                    r = e.transpose(psb(b)[:, c * 128:(c + 1) * 128], yn_b[:, c * 128:(c + 1) * 128], idb[:])
                return r
            S.op("pe", tr_yn, reads=["yn_b", "idb"], writes=[PSK(b)])

            def ev_yc(e, i=i, b=b):
                for c in range(6):
                    r = e.activation(out=mixT[:, 10 + c, i * 128:(i + 1) * 128], in_=psb(b)[:, c * 128:(c + 1) * 128], func=AF.Silu,
                                     bias=vecT[:, l, c, 3:4], scale=vecT[:, l, c, 2:3])
                return r
            S.op("act", ev_yc, reads=[PSK(b), ("vecT", l)], writes=[("mixT", 10 + c, i) for c in range(6)])

        stage(7)
        S.alias(MIXTMP_KEYS + [("hs_fm", c) for c in range(6)], MFM_KEYS)
        mixT_all = [("mixT", c, i) for c in range(16) for i in range(nt)]
        for jb in range(8):
            slot, wk = wload(wtile_cols(w_out_l, 256 * jb, 256))
            for cc in range(2):
                j = 2 * jb + cc
                b = acc()

                def mm_o(e, slot=slot, cc=cc, b=b):
                    sv = slot_view(slot, 256)
                    for kc in range(16):
                        r = e.matmul(ps[:, b, 0:N], sv[:, kc, cc * 128:(cc + 1) * 128], mixT[:, kc, 0:N], start=(kc == 0), stop=(kc == 15))
                    return r
                S.op("pe", mm_o, reads=mixT_all + [wk], writes=[PSK(b)])
                if j % 2 == 0:
                    S.op("act", lambda e, j=j, b=b: e.activation(out=m_fm[:, j, 0:N], in_=ps[:, b, 0:N], func=AF.Copy),
                         reads=[PSK(b)], writes=[("mfm", j, i) for i in range(nt)])
                else:
                    S.op("dve", lambda e, j=j, b=b: e.tensor_copy(out=m_fm[:, j, 0:N], in_=ps[:, b, 0:N]),
                         reads=[PSK(b)], writes=[("mfm", j, i) for i in range(nt)])
        postnorm_residual(tiles, m_fm, "mfm", l * 4 + 1, l)

        stage(8)
        prenorm_to_hT(tiles, l * 4 + 2)

        stage(9)
        S.alias(MFM_KEYS + MIXTMP_KEYS, HID_KEYS)
        sgbuf = [sg, u_sb]
        sgkey = ["sg", "u_sb"]
        for jb in range(22):
            slg, kg = wload(wtile_cols(w_gate_l, 256 * jb, 256))
            slu_, ku = wload(wtile_cols(w_up_l, 256 * jb, 256))
            bGs, bUs = [], []
            for cc in range(2):
                bG = acc()
                bGs.append(bG)

                def mm_g(e, slg=slg, cc=cc, bG=bG):
                    sv = slot_view(slg, 256)
                    for kc in range(16):
                        r = e.matmul(ps[:, bG, 0:N], sv[:, kc, cc * 128:(cc + 1) * 128], hT[:, kc, 0:N], start=(kc == 0), stop=(kc == 15))
                    return r
                S.op("pe", mm_g, reads=hT_all + [kg], writes=[PSK(bG)])
                S.op("act", lambda e, bG=bG, cc=cc: e.activation(out=sgbuf[cc][:, 0:N], in_=ps[:, bG, 0:N], func=AF.Silu),
                     reads=[PSK(bG)], writes=[sgkey[cc]])
            for cc in range(2):
                j = 2 * jb + cc
                bU = acc()

                def mm_u2(e, slu_=slu_, cc=cc, bU=bU):
                    sv = slot_view(slu_, 256)
                    for kc in range(16):
                        r = e.matmul(ps[:, bU, 0:N], sv[:, kc, cc * 128:(cc + 1) * 128], hT[:, kc, 0:N], start=(kc == 0), stop=(kc == 15))
                    return r
                S.op("pe", mm_u2, reads=hT_all + [ku], writes=[PSK(bU)])
                S.op("dve", lambda e, j=j, bU=bU, cc=cc: e.tensor_tensor(out=hid[:, j, 0:N], in0=sgbuf[cc][:, 0:N], in1=ps[:, bU, 0:N], op=ALU.mult),
                     reads=[sgkey[cc], PSK(bU)], writes=[("hid", j)])

        stage(10)
        S.alias(BACT_KEYS, FFM_KEYS)
        hid_all = [("hid", j) for j in range(NFC)]
        wdv = w_down_l.rearrange("(fc p) n -> p fc n", p=128)
        for j in range(16):
            halves = []
            for hh in range(2):
                halves.append(wload([(lambda s: s[:, 0:22 * 128].rearrange("p (fc n) -> p fc n", fc=22),
                                      wdv[:, hh * 22:(hh + 1) * 22, j * 128:(j + 1) * 128])]))
            b = acc()

            for hh in range(2):
                def mm_d(e, hh=hh, halves=halves, b=b):
                    sv = halves[hh][0][:, 0:22 * 128].rearrange("p (fc n) -> p fc n", fc=22)
                    for f in range(22):
                        fc = hh * 22 + f
                        r = e.matmul(ps[:, b, 0:N], sv[:, f, :], hid[:, fc, 0:N], start=(fc == 0), stop=(fc == NFC - 1))
                    return r
                S.op("pe", mm_d, reads=[("hid", hh * 22 + f) for f in range(22)] + [halves[hh][1]], writes=[PSK(b)])
            if j % 2 == 0:
                S.op("act", lambda e, j=j, b=b: e.activation(out=f_fm[:, j, 0:N], in_=ps[:, b, 0:N], func=AF.Copy),
                     reads=[PSK(b)], writes=[("ffm", j, i) for i in range(nt)])
            else:
                S.op("dve", lambda e, j=j, b=b: e.tensor_copy(out=f_fm[:, j, 0:N], in_=ps[:, b, 0:N]),
                     reads=[PSK(b)], writes=[("ffm", j, i) for i in range(nt)])
        postnorm_residual(tiles, f_fm, "ffm", l * 4 + 3, l)

    try:
        stage(0)
        for gi, tiles in enumerate(groups):
            for i, t in enumerate(tiles):
                S.op("sp", dma1(xbuf[:, i, :], xin[t]), writes=[("x", i)], dma=1)
            for l in range(DEPTH):
                layer(gi, tiles, l)
            for i, t in enumerate(tiles):
                if t == HALO_T:
                    continue
                dst = ys_d[:, :] if t == SAMPLE_T else yp_d[t - 1]
                S.op("sp", dma1(dst, xbuf[:, i, :]), reads=[("x", i)], dma=1)
    except _StopBuild:
        pass

    def fin(e):
        for s, v in S.all_dma.items():
            if s.name.startswith("dq"):
                e.wait_ge(s, v)
        return []
    S.ops["sp"].append(({}, fin, None, True))

    with stack:
        with nc.Block() as block:
            @block.sync
            def _(e):
                S.emit("sp", e)

            @block.gpsimd
            def _(e):
                S.emit("pool", e)

            @block.tensor
            def _(e):
                S.emit("pe", e)

            @block.scalar
            def _(e):
                S.emit("act", e)

            @block.vector
            def _(e):
                S.emit("dve", e)
    return nc


def _pool_mats(core):
    W = (2, 4, 8, 16)
    pm = np.zeros((20, 128, 128), np.float32)
    tp = np.arange(128)[:, None]
    t = np.arange(128)[None, :]
    for g, w in enumerate(W):
        cur = ((t - tp >= 0) & (t - tp < w)).astype(np.float32) / w - (t == tp)
        prev = ((t - (tp - 128)) < w).astype(np.float32) / w
        pm[g] = cur
        pm[4 + g] = prev
        if core == 0:
            cnt = np.minimum(t + 1, w).astype(np.float32)
            pm[8 + g] = ((t - tp >= 0) & (t - tp < w)).astype(np.float32) / cnt - (t == tp)
        else:
            pm[8 + g] = cur
        bi, ii = np.divmod(np.arange(128), 32)
        same = (bi[:, None] == bi[None, :])
        dj = ii[None, :] - ii[:, None]
        pm[12 + g] = (same & (dj >= 0) & (dj < w)).astype(np.float32) / w - np.eye(128, dtype=np.float32)
        hist = np.zeros((128, 128), np.float32)
        rb, rr = np.divmod(np.arange(60), 15)
        m = (rb[:, None] == bi[None, :]) & ((ii[None, :] - (rr[:, None] - 15)) < w)
        hist[:60] = m.astype(np.float32) / w
        pm[16 + g] = hist
    return np.ascontiguousarray(pm.transpose(1, 0, 2))


_NC_CACHE = {}


def kernel(x_prompt, x_sample, state_pool, state_conv, norm_pre_mix, norm_post_mix, norm_pre_ffn, norm_post_ffn,
           w_in, w_out, pool_w, pool_scale, sgu_norm, sgu_ws, sgu_b, conv_w, conv_b, conv_ln_g, conv_ln_b,
           w_gate, w_up, w_down):
    f = lambda a: np.ascontiguousarray(np.asarray(a, dtype=np.float32))
    x_prompt, x_sample, state_pool, state_conv = f(x_prompt), f(x_sample), f(state_pool), f(state_conv)
    xp = x_prompt[0]
    g4 = np.stack([f(norm_pre_mix), f(norm_post_mix), f(norm_pre_ffn), f(norm_post_ffn)], axis=1).reshape(8, D)
    vecs = np.zeros((DEPTH, 4, 768), np.float32)
    vecs[:, 0, :512] = f(pool_scale)
    vecs[:, 1] = f(conv_b)
    vecs[:, 2] = f(conv_ln_g)
    vecs[:, 3] = f(conv_ln_b)
    shared = {
        "ident": np.eye(128, dtype=np.float32),
        "w_in": f(w_in), "w_out": f(w_out), "w_gate": f(w_gate), "w_up": f(w_up), "w_down": f(w_down),
        "g4": np.ascontiguousarray(g4), "pool_w": f(pool_w), "vecs": vecs, "sgu_norm": f(sgu_norm),
        "sgu_ws": f(sgu_ws), "sgu_b": f(sgu_b), "conv_w": f(conv_w),
    }
    in_maps = []
    for c in range(NCORES):
        xin = np.zeros((NT, 128, D), np.float32)
        if c > 0:
            xin[0] = xp[c * 2048 - 128:c * 2048]
        xin[1:17] = xp[c * 2048:(c + 1) * 2048].reshape(16, 128, D)
        xin[17] = x_sample[4 * c:4 * c + 4].reshape(128, D)
        m = dict(shared)
        m["xin"] = xin
        m["flag"] = np.full((128, 1), 0.0 if c == 0 else 1.0, np.float32)
        m["spool"] = np.ascontiguousarray(state_pool[:, 4 * c:4 * c + 4].reshape(DEPTH, 60, 512))
        m["sconv"] = np.ascontiguousarray(state_conv[:, 4 * c:4 * c + 4].reshape(DEPTH, 120, 768))
        m["pmat"] = _pool_mats(c)
        in_maps.append(m)
    if "nc" not in _NC_CACHE:
        _NC_CACHE["nc"] = build_program()
    nc = _NC_CACHE["nc"]
    res = run_bass_kernel_spmd(nc, in_maps, core_ids=list(range(NCORES)))
    R = res.results
    y_prompt = np.concatenate([R[c]["yp"].reshape(2048, D) for c in range(NCORES)], axis=0)[None]
    y_sample = np.concatenate([R[c]["ys"].reshape(4, 32, D) for c in range(NCORES)], axis=0)
    new_pool_p = R[NCORES - 1]["npool_p"][:, None, 113:128]
    new_conv_p = R[NCORES - 1]["nconv_p"][:, None, 98:128]
    new_pool_s = np.concatenate([R[c]["npool_s"].reshape(DEPTH, 4, 32, 512)[:, :, 17:32] for c in range(NCORES)], axis=1)
    new_conv_s = np.concatenate([R[c]["nconv_s"].reshape(DEPTH, 4, 32, 768)[:, :, 2:32] for c in range(NCORES)], axis=1)
    new_v_s = np.concatenate([R[c]["nv_s"].reshape(DEPTH, 4, 32, 768) for c in range(NCORES)], axis=1)
    out = (y_prompt, y_sample, new_pool_p, new_conv_p, new_pool_s, new_conv_s, new_v_s)
    return tuple(np.ascontiguousarray(o, dtype=np.float32) for o in out)
```
